# Optimizing a Trainium2 kernel written in Bass

```python
import math
import jax, jax.numpy as jnp
from jax import lax
import numpy as np

D_MODEL = 1024
BATCH = 2
SEQ = 8192
DEPTH = 2

CHUNK = 64
Q_BLOCK = 128
EPS = 1e-6

GLA_HEADS = 4
GLA_DK = 64
GLA_DV = 128
GLA_RANK = 16
GLA_TAU = 16.0
DSA_HEADS = 4
DSA_HD = 128
IDX_HEADS = 8
IDX_HD = 64
TOPK_MAX = 256
DIFF_HEADS = 4
DIFF_HD = 64

BR_W = 512
N_BRANCH = 3

SIZES = [
    GLA_HEADS * GLA_DK, GLA_HEADS * GLA_DK, GLA_HEADS * GLA_DV, GLA_RANK, BR_W,
    DSA_HEADS * DSA_HD, DSA_HEADS * DSA_HD, DSA_HEADS * DSA_HD,
    IDX_HEADS * IDX_HD, IDX_HD, IDX_HEADS, BR_W,
    DIFF_HEADS * 2 * DIFF_HD, DIFF_HEADS * 2 * DIFF_HD, DIFF_HEADS * 2 * DIFF_HD, BR_W,
    N_BRANCH * D_MODEL,
]
N_IN = int(sum(SIZES))
SPLIT_AT = [int(s) for s in np.cumsum(SIZES)[:-1]]

kernel_name = "hybrid_gla_dsa_diff_chunk_causal"

F32 = jnp.float32


def _rms(x, g=None):
    xf = x.astype(F32)
    y = xf * lax.rsqrt(jnp.mean(xf * xf, axis=-1, keepdims=True) + EPS)
    if g is not None:
        y = y * g.astype(F32)
    return y.astype(x.dtype)


def _chunk_visible(start, n_q, n_k):
    pos_q = start + jnp.arange(n_q)
    vis_end = (pos_q // CHUNK + 1) * CHUNK
    return jnp.arange(n_k)[None, :] < vis_end[:, None]


def _to_blocks(a, nb):
    return a.reshape((a.shape[0], nb, Q_BLOCK) + a.shape[2:]).swapaxes(0, 1)


def _gla(q, k, v, a_lr, wa2, ba, norm_g):
    B, T, _ = q.shape
    nc = T // CHUNK
    shp = (B, nc, CHUNK, GLA_HEADS)
    q = q.astype(F32).reshape(shp + (GLA_DK,)) * GLA_DK ** -0.5
    k = k.astype(F32).reshape(shp + (GLA_DK,))
    v = v.astype(F32).reshape(shp + (GLA_DV,))
    log_a = jax.nn.log_sigmoid((a_lr @ wa2 + ba).astype(F32)) / GLA_TAU
    log_a = log_a.reshape(shp + (GLA_DK,))
    cum = jnp.cumsum(log_a, axis=2)
    total = cum[:, :, -1]
    k_dec = k * jnp.exp(total[:, :, None] - cum)
    u = jnp.einsum('bnchk,bnchv->nbhkv', k_dec, v)
    decay = jnp.exp(total).transpose(1, 0, 2, 3)

    def step(S, inp):
        a, uc = inp
        S = a[..., None] * S + uc
        return S, S

    S0 = jnp.zeros((B, GLA_HEADS, GLA_DK, GLA_DV), F32)
    _, states = lax.scan(step, S0, (decay, u))
    o = jnp.einsum('bnchk,nbhkv->bnchv', q, states)
    o = _rms(o, norm_g)
    return o.reshape(B, T, GLA_HEADS * GLA_DV)


def _dsa(q, k, v, iq, ik, iw, qn_g, kn_g):
    B, T, _ = q.shape
    nb = T // Q_BLOCK
    topk = min(TOPK_MAX, T // 4)
    q = _rms(q.reshape(B, T, DSA_HEADS, DSA_HD), qn_g)
    k = _rms(k.reshape(B, T, DSA_HEADS, DSA_HD), kn_g)
    v = v.reshape(B, T, DSA_HEADS, DSA_HD)
    iq = iq.reshape(B, T, IDX_HEADS, IDX_HD)
    iw = iw.astype(F32) * IDX_HEADS ** -0.5
    starts = jnp.arange(nb) * Q_BLOCK

    def one(inp):
        qb, iqb, iwb, start = inp
        allowed = _chunk_visible(start, Q_BLOCK, T)
        dots = jnp.einsum('bqhd,bsd->bqhs', iqb, ik, preferred_element_type=F32) * IDX_HD ** -0.5
        score = jnp.einsum('bqhs,bqh->bqs', jax.nn.relu(dots), iwb)
        score = jnp.where(allowed[None], score, -jnp.inf)
        top_val, top_idx = lax.top_k(score, topk)
        valid = jnp.isfinite(top_val)
        k_sel = jax.vmap(lambda kk, ii: kk[ii])(k, top_idx)
        v_sel = jax.vmap(lambda vv, ii: vv[ii])(v, top_idx)
        logits = jnp.einsum('bqhd,bqshd->bhqs', qb, k_sel, preferred_element_type=F32) * DSA_HD ** -0.5
        logits = jnp.where(valid[:, None], logits, -jnp.inf)
        p = jax.nn.softmax(logits, axis=-1).astype(v.dtype)
        return jnp.einsum('bhqs,bqshd->bqhd', p, v_sel)

    out = lax.map(one, (_to_blocks(q, nb), _to_blocks(iq, nb), _to_blocks(iw, nb), starts))
    return out.swapaxes(0, 1).reshape(B, T, DSA_HEADS * DSA_HD)


def _diff(q, k, v, qn_g, kn_g, lq1, lk1, lq2, lk2, lambda_init):
    B, T, _ = q.shape
    nb = T // Q_BLOCK
    q = _rms(q.reshape(B, T, DIFF_HEADS, 2, DIFF_HD), qn_g)
    k = _rms(k.reshape(B, T, DIFF_HEADS, 2, DIFF_HD), kn_g)
    v = v.reshape(B, T, DIFF_HEADS, 2 * DIFF_HD)
    lam = (jnp.exp(jnp.sum(lq1.astype(F32) * lk1.astype(F32)))
           - jnp.exp(jnp.sum(lq2.astype(F32) * lk2.astype(F32))) + lambda_init)
    starts = jnp.arange(nb) * Q_BLOCK

    def one(inp):
        qb, start = inp
        allowed = _chunk_visible(start, Q_BLOCK, T)
        logits = jnp.einsum('bqhcd,bshcd->bhcqs', qb, k, preferred_element_type=F32) * DIFF_HD ** -0.5
        logits = jnp.where(allowed, logits, -jnp.inf)
        p = jax.nn.softmax(logits, axis=-1)
        attn = (p[:, :, 0] - lam * p[:, :, 1]).astype(v.dtype)
        return jnp.einsum('bhqs,bshd->bqhd', attn, v)

    out = lax.map(one, (_to_blocks(q, nb), starts)).swapaxes(0, 1)
    out = _rms(out) * (1.0 - lambda_init)
    return out.reshape(B, T, DIFF_HEADS * 2 * DIFF_HD)


def setup_inputs(seed: int = 0) -> dict:
    key = jax.random.key(seed)
    ks = jax.random.split(key, 16)
    nrm = jax.random.normal
    L = DEPTH
    return {
        "x": nrm(ks[0], (BATCH, SEQ, D_MODEL), F32),
        "norm_g": 1.0 + 0.02 * nrm(ks[1], (L, D_MODEL), F32),
        "w_in": nrm(ks[2], (L, D_MODEL, N_IN), F32) * D_MODEL ** -0.5,
        "gla_wa2": nrm(ks[3], (L, GLA_RANK, GLA_HEADS * GLA_DK), F32) * GLA_RANK ** -0.5,
        "gla_ba": 1.0 + 0.1 * nrm(ks[4], (L, GLA_HEADS * GLA_DK), F32),
        "gla_norm_g": 1.0 + 0.02 * nrm(ks[5], (L, GLA_DV), F32),
        "dsa_qn_g": 1.0 + 0.02 * nrm(ks[6], (L, DSA_HD), F32),
        "dsa_kn_g": 1.0 + 0.02 * nrm(ks[7], (L, DSA_HD), F32),
        "diff_qn_g": 1.0 + 0.02 * nrm(ks[8], (L, DIFF_HD), F32),
        "diff_kn_g": 1.0 + 0.02 * nrm(ks[9], (L, DIFF_HD), F32),
        "diff_lq1": 0.1 * nrm(ks[10], (L, DIFF_HD), F32),
        "diff_lk1": 0.1 * nrm(ks[11], (L, DIFF_HD), F32),
        "diff_lq2": 0.1 * nrm(ks[12], (L, DIFF_HD), F32),
        "diff_lk2": 0.1 * nrm(ks[13], (L, DIFF_HD), F32),
        "w_br": nrm(ks[14], (L, N_BRANCH, BR_W, D_MODEL), F32) * BR_W ** -0.5,
        "w_out": nrm(ks[15], (L, D_MODEL, D_MODEL), F32) * D_MODEL ** -0.5,
    }


def reference(x, norm_g, w_in, gla_wa2, gla_ba, gla_norm_g, dsa_qn_g, dsa_kn_g, diff_qn_g, diff_kn_g,
              diff_lq1, diff_lk1, diff_lq2, diff_lk2, w_br, w_out):
    B, T, _ = x.shape
    for l in range(DEPTH):
        h = _rms(x, norm_g[l])
        proj = h @ w_in[l]
        (gq, gk, gv, ga, gz, bq, bk, bv, iq, ik, iw, bz,
         cq, ck, cv, cz, gate) = jnp.split(proj, SPLIT_AT, axis=-1)
        lambda_init = 0.8 - 0.6 * math.exp(-0.3 * l)
        y_a = _gla(gq, gk, gv, ga, gla_wa2[l], gla_ba[l], gla_norm_g[l]).astype(x.dtype) * jax.nn.silu(gz)
        y_b = _dsa(bq, bk, bv, iq, ik, iw, dsa_qn_g[l], dsa_kn_g[l]) * jax.nn.silu(bz)
        y_c = _diff(cq, ck, cv, diff_qn_g[l], diff_kn_g[l], diff_lq1[l], diff_lk1[l],
                    diff_lq2[l], diff_lk2[l], lambda_init) * jax.nn.silu(cz)
        g = jax.nn.sigmoid(gate.reshape(B, T, N_BRANCH, D_MODEL))
        merged = (g[:, :, 0] * (y_a @ w_br[l, 0])
                  + g[:, :, 1] * (y_b @ w_br[l, 1])
                  + g[:, :, 2] * (y_c @ w_br[l, 2]))
        x = x + merged @ w_out[l]
    return x
```

```python
import numpy as np
from contextlib import ExitStack
import concourse.bass as bass
import concourse.mybir as mybir
from concourse.bass_utils import run_bass_kernel_spmd

F32 = mybir.dt.float32
BF16 = mybir.dt.bfloat16
AF = mybir.ActivationFunctionType
ALU = mybir.AluOpType
AX = mybir.AxisListType

NDS = 8
K_IT = 20
TOPK = 256.0
NEG = -30000.0


import types


def _freeze(fn):
    if fn is None or fn.__closure__ is None:
        return fn
    cells = []
    for c in fn.__closure__:
        try:
            cells.append(types.CellType(c.cell_contents))
        except ValueError:
            cells.append(c)
    g = types.FunctionType(fn.__code__, fn.__globals__, fn.__name__, fn.__defaults__, tuple(cells))
    g.__kwdefaults__ = fn.__kwdefaults__
    return g


class Res:
    __slots__ = ("name", "w", "r")

    def __init__(self, name):
        self.name = name
        self.w = None
        self.r = {}


class Sched:
    def __init__(self, nc):
        self.nc = nc
        self.ops = []
        self.cnt = {k: 0 for k in ("pe", "act", "dve", "pool")}
        self.last = {}
        self.dslot = {q: 0 for q in ("sp", "act", "pool")}
        self.dcnt = {q: [0] * NDS for q in ("sp", "act", "pool")}
        self.dlast = {q: [None] * NDS for q in ("sp", "act", "pool")}
        self.nres = 0

    def res(self, name=None):
        self.nres += 1
        return Res(name or f"r{self.nres}")

    def _deps(self, engine, reads, writes, is_dma):
        deps = []
        for r in reads:
            if r.w is not None:
                deps.append(r.w)
        for w in writes:
            if w.w is not None and (is_dma or w.w[0] != ("c", engine)):
                deps.append(w.w)
            for e, t in w.r.items():
                if is_dma or e != engine:
                    deps.append(t)
        return deps

    def op(self, engine, fn, reads=(), writes=()):
        deps = self._deps(engine, reads, writes, False)
        self.cnt[engine] += 1
        tok = (("c", engine), self.cnt[engine])
        self.ops.append([engine, _freeze(fn), deps, tok, False])
        for r in reads:
            r.r[engine] = tok
        for w in writes:
            w.w = tok
            w.r = {}
        self.last[("c", engine)] = tok
        return tok

    def dma(self, q, fn, reads=(), writes=()):
        deps = self._deps(q, reads, writes, True)
        slot = self.dslot[q]
        self.dslot[q] = (slot + 1) % NDS
        if self.dlast[q][slot] is not None:
            deps.append(self.dlast[q][slot])
        self.dcnt[q][slot] += 16
        tok = (("d", q, slot), self.dcnt[q][slot])
        self.dlast[q][slot] = tok
        self.ops.append([q, _freeze(fn), deps, tok, True])
        for r in reads:
            r.r[("dma", q, slot)] = tok
        for w in writes:
            w.w = tok
            w.r = {}
        self.last[("d", q, slot)] = tok
        return tok

    def fence(self):
        deps = list(self.last.values())
        for e in ("pe", "act", "dve", "pool", "sp"):
            self.ops.append([e, None, list(deps), None, False])

    def wait_all(self, engine, resources):
        deps = [r.w for r in resources if r.w is not None]
        self.ops.append([engine, None, deps, None, False])

    def finalize(self):
        nc = self.nc
        engs = {"pe": nc.tensor, "act": nc.scalar, "dve": nc.vector, "pool": nc.gpsimd, "sp": nc.sync}
        needed = set()
        for engine, fn, deps, tok, is_dma in self.ops:
            for d in deps:
                if d[0][0] == "c":
                    needed.add(d)
        newval = {}
        run = {k: 0 for k in self.cnt}
        for engine, fn, deps, tok, is_dma in self.ops:
            if tok is not None and tok[0][0] == "c":
                if tok in needed:
                    run[tok[0][1]] += 1
                newval[tok] = run[tok[0][1]]
        sems = {}

        def sem_of(key):
            if key not in sems:
                sems[key] = nc.alloc_semaphore(name="s_" + "_".join(str(x) for x in key))
            return sems[key]

        seen = {k: {} for k in engs}
        for engine, fn, deps, tok, is_dma in self.ops:
            e = engs[engine]
            mx = {}
            for d in deps:
                key, val = d
                if key[0] == "c":
                    val = newval[d]
                if val > mx.get(key, 0):
                    mx[key] = val
            for key, val in mx.items():
                if seen[engine].get(key, 0) >= val:
                    continue
                e.wait_ge(sem_of(key), val)
                seen[engine][key] = val
            if fn is None:
                continue
            ins = fn(e)
            if is_dma:
                ins.then_inc(sem_of(tok[0]), 16)
            elif tok in needed:
                ins.then_inc(sem_of(tok[0]), 1)
        return len(self.ops)


_UID = [0]


class Ring:
    def __init__(self, es, nc, S, name, shape, dtype, n, psum=False):
        self.items = []
        _UID[0] += 1
        name = f"{name}_u{_UID[0]}_"
        for i in range(n):
            if psum:
                t = es.enter_context(nc.psum_tensor(f"{name}{i}", shape, dtype)).ap()
            else:
                t = es.enter_context(nc.sbuf_tensor(f"{name}{i}", shape, dtype)).ap()
            self.items.append((t, S.res(f"{name}{i}")))
        self.i = 0

    def next(self):
        it = self.items[self.i % len(self.items)]
        self.i += 1
        return it


_NAMES = ["gq", "gk", "gv", "ga", "gz", "bq", "bk", "bv", "iq", "ik", "iw", "bz", "cq", "ck", "cv", "cz", "gate"]
_SIZES = [256, 256, 512, 16, 512, 512, 512, 512, 512, 64, 8, 512, 512, 512, 512, 512, 3072]
_OFF = {}
_o = 0
for _n, _s in zip(_NAMES, _SIZES):
    _OFF[_n] = _o
    _o += _s


def _rng(n, a=0, b=None):
    i = _NAMES.index(n)
    b = _SIZES[i] if b is None else b
    return list(range(_OFF[n] + a, _OFF[n] + b))


def wk_cols():
    c = _rng("ck") + _rng("bk") + _rng("ik") + _rng("ik") + _rng("ga") + [0] * 112
    c += _rng("gk") + _rng("gv") + _rng("bv") + _rng("cv")
    return np.array(c, dtype=np.int64)


NKF = 10 * 128
NK = NKF + 1792


def wq_cols():
    c = _rng("gq") + _rng("bq") + _rng("iq") + _rng("cq") + _rng("gz") + _rng("bz") + _rng("cz") + _rng("gate")
    c += _rng("iw") + [0] * 120
    return np.array(c, dtype=np.int64)


NQF = 256 + 512 * 6 + 3072
NQ = NQF + 128


def build_fused(nc, layers=(0, 1), fused=True, debug=False):
    S = Sched(nc)
    di = lambda n, s, d=F32: nc.dram_tensor(n, s, d, kind="ExternalInput").ap()
    dsc = lambda n, s, d=BF16: nc.dram_tensor(n, s, d, kind="Internal").ap()
    xT_full = di("xT_full", [1024, 8192])
    W = {}
    for l in layers:
        W[l] = {"w_k": di(f"w_k{l}", [1024, NK]), "w_q": di(f"w_q{l}", [1024, NQ]), "w_br": di(f"w_br{l}", [3, 512, 1024]),
                "w_out": di(f"w_out{l}", [1024, 1024]), "cst": di(f"cst{l}", [128, 16]), "wa2": di(f"wa2{l}", [16, 256]),
                "ba_bc": di(f"ba_bc{l}", [128, 256]), "lqk": di(f"lqk{l}", [128, 256])}
    cmat = di("cmat", [128, 4, 128])
    ind = di("ind", [128, 2])
    iota = di("iota", [128, 2048])
    NV = 5
    cvar = {"vis_rel": di("vis_rel", [NV, 128, 16]), "visA": di("visA", [NV, 128, 4, 512]), "valid": di("valid", [NV, 128, 128])}
    sel_in = di("sel", [128, 4])
    pow2_in = di("pow2", [128, K_IT])
    outT = nc.dram_tensor("outT", [1024, 2048], F32, kind="ExternalOutput").ap()
    D_x1T = dsc("D_x1T", [1024, 8192], F32); R_x1T = S.res()
    dbg = {}

    D_ckT = dsc("D_ckT", [4, 128, 8192]); D_bkT = dsc("D_bkT", [4, 128, 8192])
    D_cv = dsc("D_cv", [8192, 512]); D_bv = dsc("D_bv", [8192, 512])
    D_kdec = dsc("D_kdec", [8192, 256]); D_gv = dsc("D_gv", [8192, 512])
    D_cqT = dsc("D_cqT", [4, 128, 2048]); D_bqT = dsc("D_bqT", [4, 128, 2048])
    D_iqT = dsc("D_iqT", [4, 128, 2048]); D_gqT = dsc("D_gqT", [2, 128, 2048])
    D_zT = dsc("D_zT", [3, 4, 128, 2048]); D_gT = dsc("D_gT", [24, 128, 2048], F32)
    D_yT = dbg["yT"] if debug else dsc("D_yT", [3, 4, 128, 2048])
    D_nm = dsc("D_nm", [4, 64, 128, 512])
    R_ckT = [S.res() for _ in range(4)]; R_bkT = [S.res() for _ in range(4)]
    R_cv = S.res(); R_bv = S.res(); R_kdec = S.res(); R_gv = S.res()
    R_cqT = S.res(); R_bqT = S.res(); R_iqT = S.res(); R_gqT = S.res(); R_zT = S.res(); R_gT = S.res()
    R_yT = [[S.res() for _ in range(4)] for _ in range(3)]
    R_nm = [S.res() for _ in range(16)]
    outs = []

    def newout():
        r = S.res(); outs.append(r); return r

    top = ExitStack()
    def sbt(es, n, s, d=F32):
        _UID[0] += 1
        return es.enter_context(nc.sbuf_tensor(f"{n}_u{_UID[0]}", s, d)).ap()

    cst_t = sbt(top, "cst_t", [128, 16]); r_cst = S.res()
    cm = sbt(top, "cm", [128, 4, 128]); r_cm = S.res()
    cmb = sbt(top, "cmb", [128, 4, 128], BF16); r_cmb = S.res()
    ind_t = sbt(top, "ind_t", [128, 2]); r_ind = S.res()
    ikT = sbt(top, "ikT", [128, 8192], BF16); r_ikT = S.res()
    dec = sbt(top, "dec", [64, 4, 128]); r_dec = S.res()
    iw_t = sbt(top, "iw_t", [128, 16, 8]); r_iw = S.res()
    lam = sbt(top, "lam", [128, 4]); r_lam = S.res()
    S.dma("sp", lambda e: e.dma_start(out=cm, in_=cmat), writes=[r_cm])
    S.dma("sp", lambda e: e.dma_start(out=ind_t, in_=ind), writes=[r_ind])
    S.op("dve", lambda e: e.tensor_copy(out=cmb, in_=cm), reads=[r_cm], writes=[r_cmb])
    ones_f = cm[:, 0, :]; bones_f = cm[:, 1, :]; mdec_f = cm[:, 2, :]
    ones_b = cmb[:, 0, :]; ident_b = cmb[:, 3, :]
    eps_t = sbt(top, "eps_t", [128, 4]); r_eps = S.res()
    S.op("dve", lambda e: e.memset(eps_t[:, 0:1], 1e-6), writes=[r_eps])
    S.op("dve", lambda e: e.memset(eps_t[:, 1:2], 1e-6 * 64.0), writes=[r_eps])
    S.op("dve", lambda e: e.memset(eps_t[:, 2:3], 1e-6 * 128.0), writes=[r_eps])
    ebias_t = eps_t[:, 3:4]
    S.op("dve", lambda e: e.memset(eps_t[:, 3:4], -8.0), writes=[r_eps])
    sel_t = sbt(top, "sel_t", [128, 4]); r_sel = S.res()
    S.dma("sp", lambda e: e.dma_start(out=sel_t, in_=sel_in), writes=[r_sel])

    def layer_setup(l):
        S.dma("sp", lambda e: e.dma_start(out=cst_t, in_=W[l]["cst"]), writes=[r_cst])

        with ExitStack() as es:
            lq = sbt(es, "lq", [128, 256]); r_lq = S.res()
            pr = sbt(es, "lpr", [128, 128]); r_pr = S.res()
            sm = sbt(es, "lsm", [128, 2]); r_sm = S.res()
            S.dma("sp", lambda e: e.dma_start(out=lq, in_=W[l]["lqk"]), writes=[r_lq])
            lqv = lq.rearrange("p (a b d) -> p a b d", a=2, b=2)
            S.op("dve", lambda e: e.tensor_tensor(out=pr.rearrange("p (a d) -> p a d", a=2), in0=lqv[:, :, 0, :], in1=lqv[:, :, 1, :], op=ALU.mult), reads=[r_lq], writes=[r_pr])
            S.op("dve", lambda e: e.tensor_reduce(out=sm, in_=pr.rearrange("p (a d) -> p a d", a=2), axis=AX.X, op=ALU.add), reads=[r_pr], writes=[r_sm])
            S.op("act", lambda e: e.activation(out=sm, in_=sm, func=AF.Exp), reads=[r_sm], writes=[r_sm])
            S.op("dve", lambda e: e.tensor_tensor(out=lam[:, 0:1], in0=sm[:, 1:2], in1=sm[:, 0:1], op=ALU.subtract), reads=[r_sm], writes=[r_lam])
            S.op("dve", lambda e: e.tensor_tensor(out=lam[:, 0:1], in0=lam[:, 0:1], in1=cst_t[:, 13:14], op=ALU.subtract), reads=[r_lam, r_cst], writes=[r_lam])
            S.op("dve", lambda e: e.tensor_scalar(out=lam[:, 1:2], in0=cst_t[:, 13:14], scalar1=-1.0, scalar2=1.0, op0=ALU.mult, op1=ALU.add), reads=[r_cst, r_lam], writes=[r_lam])
            S.fence()

    def make_h(es, xloader, ntg, hT, r_hT, multi=False):
        xs = Ring(es, nc, S, "xs", [128, 8, 512], F32, 2)
        xc = Ring(es, nc, S, "xc", [128, 8, 512], F32, 1)
        sq = Ring(es, nc, S, "sq", [128, 8, 512], F32, 1)
        pss = Ring(es, nc, S, "pss", [128, 512], F32, 2, psum=True)
        rb = Ring(es, nc, S, "rb", [128, 512], F32, 2)
        for tg in range(ntg):
            x_t, rx = xs.next()
            xloader(x_t, rx, tg, xc)
            s_t, rsq = sq.next()
            S.op("act", lambda e, x_t=x_t, s_t=s_t: e.activation(out=s_t, in_=x_t, func=AF.Square), reads=[rx], writes=[rsq])
            p, rp = pss.next()
            for kc in range(8):
                S.op("pe", lambda e, p=p, s_t=s_t, kc=kc: e.matmul(p, lhsT=ones_f, rhs=s_t[:, kc, :], start=(kc == 0), stop=(kc == 7)), reads=[rsq, r_cm], writes=[rp])
            r_t, rr = rb.next()
            S.op("act", lambda e, p=p, r_t=r_t: e.activation(out=r_t, in_=p, func=AF.Ln, scale=1.0 / 1024, bias=eps_t[:, 0:1]), reads=[rp, r_eps], writes=[rr])
            S.op("act", lambda e, r_t=r_t: e.activation(out=r_t, in_=r_t, func=AF.Exp, scale=-0.5), reads=[rr], writes=[rr])
            g = tg if multi else 0
            S.op("dve", lambda e, x_t=x_t, r_t=r_t, g=g: e.tensor_tensor(out=hT[:, :, g * 512:(g + 1) * 512], in0=x_t, in1=r_t.rearrange("p (o t) -> p o t", o=1).to_broadcast([128, 8, 512]), op=ALU.mult), reads=[rx, rr], writes=[r_hT])
            yield tg

    def load_w(es, wsrc, c0, n, name, r_w):
        wb = sbt(es, name, [128, 8, n], BF16)
        wv = wsrc.rearrange("(kc p) n -> p kc n", p=128)
        with ExitStack() as es2:
            ws = Ring(es2, nc, S, name + "_st", [128, 8, 512], F32, 2)
            for a in range(0, n, 512):
                m = min(512, n - a)
                w_t, rw = ws.next()
                S.dma("act", lambda e, w_t=w_t, a=a, m=m: e.dma_start(out=w_t[:, :, 0:m], in_=wv[:, :, c0 + a:c0 + a + m]), writes=[rw])
                for kc in range(8):
                    if kc % 2:
                        S.op("act", lambda e, w_t=w_t, kc=kc, a=a, m=m: e.activation(out=wb[:, kc, a:a + m], in_=w_t[:, kc, 0:m], func=AF.Copy, scale=cst_t[:, kc:kc + 1]), reads=[rw, r_cst], writes=[r_w])
                    else:
                        S.op("dve", lambda e, w_t=w_t, kc=kc, a=a, m=m: e.tensor_scalar(out=wb[:, kc, a:a + m], in0=w_t[:, kc, 0:m], scalar1=cst_t[:, kc:kc + 1], scalar2=None, op0=ALU.mult), reads=[rw, r_cst], writes=[r_w])
            S.fence()
        return wb

    def fm_norm(rings, p, rp, hd, gidx, scale, dst, rdst):
        sqr, pn, rr = rings
        s_t, rsq = sqr.next()
        S.op("act", lambda e: e.activation(out=s_t, in_=p, func=AF.Square), reads=[rp], writes=[rsq])
        p2, rp2 = pn.next()
        mat = ones_f if hd == 128 else bones_f
        S.op("pe", lambda e: e.matmul(p2, lhsT=mat, rhs=s_t, start=True, stop=True), reads=[rsq, r_cm], writes=[rp2])
        r_t, rrr = rr.next()
        ei = {0.125: 1, 128 ** -0.5: 2}.get(scale, 0)
        S.op("act", lambda e: e.activation(out=r_t, in_=p2, func=AF.Ln, scale=1.0 / (hd * scale * scale), bias=eps_t[:, ei:ei + 1]), reads=[rp2, r_eps], writes=[rrr])
        S.op("act", lambda e: e.activation(out=r_t, in_=r_t, func=AF.Exp, scale=-0.5), reads=[rrr], writes=[rrr])
        if gidx is None:
            S.op("dve", lambda e: e.tensor_tensor(out=dst, in0=p, in1=r_t, op=ALU.mult), reads=[rp, rrr], writes=[rdst])
        else:
            S.op("dve", lambda e: e.scalar_tensor_tensor(out=dst, in0=p, scalar=cst_t[:, gidx:gidx + 1], in1=r_t, op0=ALU.mult, op1=ALU.mult), reads=[rp, rrr, r_cst], writes=[rdst])

    def kside(l, xloader):
        with ExitStack() as es:
            r_wk = S.res()
            wkb = load_w(es, W[l]["w_k"], 0, NK, "wkb", r_wk)
            wa2_t = sbt(es, "wa2_t", [16, 256]); r_wa2 = S.res()
            ba_t = sbt(es, "ba_t", [128, 256]); r_ba = S.res()
            S.dma("sp", lambda e: e.dma_start(out=wa2_t, in_=W[l]["wa2"]), writes=[r_wa2])
            S.dma("sp", lambda e: e.dma_start(out=ba_t, in_=W[l]["ba_bc"]), writes=[r_ba])
            hT = sbt(es, "hT1", [128, 8, 512], BF16); r_hT = S.res()
            ps = Ring(es, nc, S, "p1", [128, 512], F32, 3, psum=True)
            pn = Ring(es, nc, S, "p1n", [128, 512], F32, 2, psum=True)
            sqr = Ring(es, nc, S, "n_sq", [128, 512], F32, 2)
            rr = Ring(es, nc, S, "n_r", [128, 512], F32, 2)
            stb = Ring(es, nc, S, "stb1", [128, 512], BF16, 4)
            gaT = Ring(es, nc, S, "gaT", [16, 512], F32, 2)
            tz = Ring(es, nc, S, "tz", [128, 256], F32, 2)
            tl = Ring(es, nc, S, "tl", [128, 256], F32, 2)
            te = Ring(es, nc, S, "te", [128, 256], F32, 2)
            dsm = Ring(es, nc, S, "dsm", [128, 4], F32, 2)
            for tg in make_h(es, xloader, 16, hT, r_hT):
                tsl = slice(tg * 512, (tg + 1) * 512)
                for ch in range(10):
                    p, rp = ps.next()
                    for kc in range(8):
                        S.op("pe", lambda e, p=p, kc=kc, ch=ch: e.matmul(p, lhsT=wkb[:, kc, ch * 128:(ch + 1) * 128], rhs=hT[:, kc, :], start=(kc == 0), stop=(kc == 7)), reads=[r_wk, r_hT], writes=[rp])
                    if ch < 8:
                        so, rso = stb.next()
                        if ch < 4:
                            fm_norm((sqr, pn, rr), p, rp, 64, 8, 1.0, so, rso)
                            S.dma("pool", lambda e, so=so, ch=ch, tsl=tsl: e.dma_start(out=D_ckT[ch, :, tsl], in_=so), reads=[rso], writes=[R_ckT[ch]])
                        else:
                            fm_norm((sqr, pn, rr), p, rp, 128, 9, 1.0, so, rso)
                            S.dma("pool", lambda e, so=so, ch=ch, tsl=tsl: e.dma_start(out=D_bkT[ch - 4, :, tsl], in_=so), reads=[rso], writes=[R_bkT[ch - 4]])
                    elif ch == 8:
                        S.op("act", lambda e, p=p, tsl=tsl: e.activation(out=ikT[:, tsl], in_=p, func=AF.Copy), reads=[rp], writes=[r_ikT])
                    else:
                        g_t, rg = gaT.next()
                        S.op("act", lambda e, p=p, g_t=g_t: e.activation(out=g_t, in_=p[0:16, :], func=AF.Copy), reads=[rp], writes=[rg])
                for bl in range(4):
                    blk = tg * 4 + bl
                    rows = slice(blk * 128, (blk + 1) * 128)
                    lt = hT[:, :, bl * 128:(bl + 1) * 128]
                    p2, rp2 = pn.next()
                    S.op("pe", lambda e, p2=p2, g_t=g_t, bl=bl: e.matmul(p2[:, 0:256], lhsT=g_t[:, bl * 128:(bl + 1) * 128], rhs=wa2_t, start=True, stop=True), reads=[rg, r_wa2], writes=[rp2])
                    z_t, rz = tz.next()
                    S.op("dve", lambda e, z_t=z_t, p2=p2: e.tensor_tensor(out=z_t, in0=p2[:, 0:256], in1=ba_t, op=ALU.add), reads=[rp2, r_ba], writes=[rz])
                    e_t, re_ = te.next()
                    S.op("act", lambda e, z_t=z_t, e_t=e_t: e.activation(out=e_t, in_=z_t, func=AF.Exp, scale=-1.0), reads=[rz], writes=[re_])
                    l_t, rl = tl.next()
                    S.op("act", lambda e, e_t=e_t, l_t=l_t: e.activation(out=l_t, in_=e_t, func=AF.Ln, bias=1.0), reads=[re_], writes=[rl])
                    for ti, (c0, n) in enumerate(((0, 256), (256, 512), (768, 512), (1280, 512))):
                        p, rp = ps.next()
                        for kc in range(8):
                            S.op("pe", lambda e, p=p, kc=kc, lt=lt, c0=c0, n=n: e.matmul(p[:, 0:n], lhsT=lt[:, kc, :], rhs=wkb[:, kc, NKF + c0:NKF + c0 + n], start=(kc == 0), stop=(kc == 7)), reads=[r_wk, r_hT], writes=[rp])
                        so, rso = stb.next()
                        if ti == 0:
                            p3, rp3 = pn.next()
                            S.op("pe", lambda e, p3=p3, l_t=l_t: e.matmul(p3[:, 0:256], lhsT=mdec_f, rhs=l_t, start=True, stop=True), reads=[r_cm, rl], writes=[rp3])
                            e2, re2 = te.next()
                            S.op("act", lambda e, e2=e2, p3=p3: e.activation(out=e2, in_=p3[:, 0:256], func=AF.Exp, scale=-1.0 / 16), reads=[rp3], writes=[re2])
                            S.op("dve", lambda e, so=so, p=p, e2=e2: e.tensor_tensor(out=so[:, 0:256], in0=p[:, 0:256], in1=e2, op=ALU.mult), reads=[rp, re2], writes=[rso])
                            S.dma("pool", lambda e, so=so, rows=rows: e.dma_start(out=D_kdec[rows, :], in_=so[:, 0:256]), reads=[rso], writes=[R_kdec])
                            p4, rp4 = pn.next()
                            for pr_ in range(4):
                                S.op("pe", lambda e, p4=p4, l_t=l_t, pr_=pr_: e.matmul(p4[0:64, pr_ * 2:pr_ * 2 + 2], lhsT=l_t[:, pr_ * 64:(pr_ + 1) * 64], rhs=ind_t, start=True, stop=True), reads=[rl, r_ind], writes=[rp4])
                            S.op("act", lambda e, p4=p4, blk=blk: e.activation(out=dec[:, :, blk * 2:blk * 2 + 2], in_=p4[0:64, 0:8].rearrange("p (a c) -> p a c", a=4), func=AF.Exp, scale=-1.0 / 16), reads=[rp4], writes=[r_dec])
                        else:
                            S.op("act", lambda e, so=so, p=p: e.activation(out=so, in_=p, func=AF.Copy), reads=[rp], writes=[rso])
                            dst, rd = ((D_gv, R_gv), (D_bv, R_bv), (D_cv, R_cv))[ti - 1]
                            S.dma("pool", lambda e, so=so, rows=rows, dst=dst: e.dma_start(out=dst[rows, :], in_=so), reads=[rso], writes=[rd])
            S.fence()

    def branch_epilogue(es_r, br, h, m, oT, r_oT, do_norm, gidx, post):
        sqr, pn, rr, zr, yo = es_r
        tsl = slice(m * 512, (m + 1) * 512)
        z_t, rz = zr.next()
        S.dma("sp", lambda e: e.dma_start(out=z_t, in_=D_zT[br, h, :, tsl]), reads=[R_zT], writes=[rz])
        y_t, ry = yo.next()
        if do_norm:
            s_t, rsq = sqr.next()
            S.op("act", lambda e: e.activation(out=s_t, in_=oT, func=AF.Square), reads=[r_oT], writes=[rsq])
            p2, rp2 = pn.next()
            S.op("pe", lambda e: e.matmul(p2, lhsT=ones_f, rhs=s_t, start=True, stop=True), reads=[rsq, r_cm], writes=[rp2])
            r_t, rrr = rr.next()
            S.op("act", lambda e: e.activation(out=r_t, in_=p2, func=AF.Ln, scale=1.0 / 128, bias=eps_t[:, 0:1]), reads=[rp2, r_eps], writes=[rrr])
            S.op("act", lambda e: e.activation(out=r_t, in_=r_t, func=AF.Exp, scale=-0.5), reads=[rrr], writes=[rrr])
            sc = cst_t[:, gidx:gidx + 1] if gidx is not None else lam[:, 1:2]
            S.op("dve", lambda e: e.scalar_tensor_tensor(out=r_t, in0=oT, scalar=sc, in1=r_t, op0=ALU.mult, op1=ALU.mult), reads=[r_oT, rrr, r_cst, r_lam], writes=[rrr])
            S.op("dve", lambda e: e.tensor_tensor(out=y_t, in0=r_t, in1=z_t, op=ALU.mult), reads=[rrr, rz], writes=[ry])
        else:
            S.op("dve", lambda e: e.tensor_tensor(out=y_t, in0=oT, in1=z_t, op=ALU.mult), reads=[r_oT, rz], writes=[ry])
        S.dma("pool", lambda e: e.dma_start(out=D_yT[br, h, :, tsl], in_=y_t), reads=[ry], writes=[R_yT[br][h]])

    def epi_rings(es, pfx):
        return (Ring(es, nc, S, pfx + "sq", [128, 512], F32, 2), Ring(es, nc, S, pfx + "pn", [128, 512], F32, 1, psum=True),
                Ring(es, nc, S, pfx + "r", [128, 512], F32, 2), Ring(es, nc, S, pfx + "z", [128, 512], BF16, 2),
                Ring(es, nc, S, pfx + "y", [128, 512], BF16, 2))

    def attention(es, v, br, kT_src, R_kT, v_src, R_v, qT_src, R_qT, ncomp, masked_dsa):
        er = epi_rings(es, f"a{br}_")
        kt_r = Ring(es, nc, S, f"a{br}_k", [128, 8192], BF16, 1)
        v_r = Ring(es, nc, S, f"a{br}_v", [128, 64, 128], BF16, 1)
        q_r = Ring(es, nc, S, f"a{br}_q", [128, ncomp, 2048], BF16, 1)
        if ncomp == 2:
            q0_, rq0_ = q_r.items[0]
            S.op("dve", lambda e: e.memset(q0_, 0.0), writes=[rq0_])
        va = sbt(es, f"a{br}_vis", [128, 4, 512]); r_va = S.res()
        S.dma("sp", lambda e: e.dma_start(out=va, in_=cvar["visA"][v]), writes=[r_va])
        pS = Ring(es, nc, S, f"a{br}_pS", [128, 512], F32, 4 if masked_dsa else 3, psum=True)
        pN = [Ring(es, nc, S, f"a{br}_pN{c}", [128, 512], F32, 1, psum=True) for c in range(ncomp)]
        pD = [Ring(es, nc, S, f"a{br}_pD{c}", [128, 512], F32, 1, psum=True) for c in range(ncomp)]
        LOOKAHEAD = 3 if masked_dsa else 2
        PT = Ring(es, nc, S, f"a{br}_PT", [128, 512], BF16, 6)
        MK = Ring(es, nc, S, f"a{br}_MK", [128, 512], BF16, 8 if masked_dsa else 4)
        oT_r = Ring(es, nc, S, f"a{br}_oT", [128, 512], F32, 2)
        t1_r = Ring(es, nc, S, f"a{br}_t1", [128, 512], F32, 2)
        vv = v_src.rearrange("(kt p) c -> p kt c", p=128)
        KD = 128 // ncomp
        for h in range(4):
            k_t, rk = kt_r.next(); v_t, rv = v_r.next(); q_t, rq = q_r.next()
            for a in range(4):
                S.dma("sp", lambda e, k_t=k_t, h=h, a=a: e.dma_start(out=k_t[:, a * 2048:(a + 1) * 2048], in_=kT_src[h, :, a * 2048:(a + 1) * 2048]), reads=[R_kT[h]], writes=[rk])
                S.dma("act", lambda e, v_t=v_t, h=h, a=a: e.dma_start(out=v_t[:, a * 16:(a + 1) * 16, :], in_=vv[:, a * 16:(a + 1) * 16, h * 128:(h + 1) * 128]), reads=[R_v], writes=[rv])
            if ncomp == 2:
                for c_ in range(2):
                    S.dma("sp", lambda e, q_t=q_t, h=h, c_=c_: e.dma_start(out=q_t[c_ * 64:(c_ + 1) * 64, c_, :], in_=qT_src[h, c_ * 64:(c_ + 1) * 64, :]), reads=[R_qT], writes=[rq])
            else:
                S.dma("sp", lambda e, q_t=q_t, h=h: e.dma_start(out=q_t[:, 0, :], in_=qT_src[h]), reads=[R_qT], writes=[rq])
            for m in range(4):
                nkt = 16 * m + 16 if v == 4 else 4 * (4 * m + v) + 4
                kt_mask0 = 16 * m if v == 4 else 4 * (4 * m + v)
                accs = [(pN[c].next(), pD[c].next()) for c in range(ncomp)]
                pend = []

                def back(item):
                    kt_, c_, p_t_, rpt_ = item
                    (n_, rn), (d_, rd) = accs[c_]
                    S.op("pe", lambda e: e.matmul(n_, lhsT=v_t[:, kt_, :], rhs=p_t_, start=(kt_ == 0), stop=(kt_ == nkt - 1)), reads=[rv, rpt_], writes=[rn])
                    S.op("pe", lambda e: e.matmul(d_, lhsT=ones_b, rhs=p_t_, start=(kt_ == 0), stop=(kt_ == nkt - 1)), reads=[r_cmb, rpt_], writes=[rd])

                for kt in range(nkt):
                    need_mask = (kt >= kt_mask0)
                    if masked_dsa:
                        mk, rmk = MK.next()
                        S.dma("sp", lambda e, mk=mk, m=m, kt=kt: e.dma_start(out=mk, in_=D_nm[m, kt]), reads=[R_nm[m * 4 + a] for a in range(4)], writes=[rmk])
                    elif need_mask:
                        mk, rmk = MK.next()
                        S.op("dve", lambda e, mk=mk, m=m, kt=kt: e.tensor_scalar(out=mk, in0=va[:, m, :], scalar1=float((kt - 16 * m) * 128), scalar2=NEG, op0=ALU.is_le, op1=ALU.mult), reads=[r_va], writes=[rmk])
                    for c in range(ncomp):
                        s, rs = pS.next()
                        use_mask = masked_dsa or need_mask
                        S.op("pe", lambda e, s=s, c=c, kt=kt, m=m, k_t=k_t, q_t=q_t, use_mask=use_mask: e.matmul(s, lhsT=k_t[:, kt * 128:(kt + 1) * 128], rhs=q_t[:, c, m * 512:(m + 1) * 512], start=True, stop=not use_mask), reads=[rk, rq], writes=[rs])
                        if use_mask:
                            S.op("pe", lambda e, s=s, mk=mk: e.matmul(s, lhsT=ident_b, rhs=mk, start=False, stop=True), reads=[rmk, r_cmb], writes=[rs])
                        p_t, rpt = PT.next()
                        S.op("act", lambda e, s=s, p_t=p_t: e.activation(out=p_t, in_=s, func=AF.Exp, bias=ebias_t[:, 0:1]), reads=[rs, r_eps], writes=[rpt])
                        pend.append((kt, c, p_t, rpt))
                        if len(pend) > LOOKAHEAD:
                            back(pend.pop(0))
                while pend:
                    back(pend.pop(0))
                o_t, ro = oT_r.next()
                (n_, rn), (d_, rd) = accs[0]
                t1, rt1 = t1_r.next()
                S.op("act", lambda e, t1=t1, d_=d_: e.activation(out=t1, in_=d_, func=AF.Ln), reads=[rd], writes=[rt1])
                S.op("act", lambda e, t1=t1: e.activation(out=t1, in_=t1, func=AF.Exp, scale=-1.0), reads=[rt1], writes=[rt1])
                S.op("dve", lambda e, o_t=o_t, n_=n_, t1=t1: e.tensor_tensor(out=o_t, in0=n_, in1=t1, op=ALU.mult), reads=[rn, rt1], writes=[ro])
                if ncomp == 2:
                    (n1, rn1), (d1, rd1) = accs[1]
                    t2, rt2 = t1_r.next()
                    S.op("act", lambda e, t2=t2, d1=d1: e.activation(out=t2, in_=d1, func=AF.Ln), reads=[rd1], writes=[rt2])
                    S.op("act", lambda e, t2=t2: e.activation(out=t2, in_=t2, func=AF.Exp, scale=-1.0), reads=[rt2], writes=[rt2])
                    S.op("dve", lambda e, t2=t2, n1=n1: e.tensor_tensor(out=t2, in0=n1, in1=t2, op=ALU.mult), reads=[rn1, rt2], writes=[rt2])
                    S.op("dve", lambda e, o_t=o_t, t2=t2: e.scalar_tensor_tensor(out=o_t, in0=t2, scalar=lam[:, 0:1], in1=o_t, op0=ALU.mult, op1=ALU.add), reads=[rt2, ro, r_lam], writes=[ro])
                branch_epilogue(er, br, h, m, o_t, ro, ncomp == 2, None, None)
        S.fence()

    def rest(l, v, xloader, writer):
        with ExitStack() as es:
            hT = sbt(es, "hT2", [128, 8, 2048], BF16); r_hT = S.res()
            with ExitStack() as es2:
                for _ in make_h(es2, xloader, 4, hT, r_hT, multi=True):
                    pass
                S.fence()
            ps = Ring(es, nc, S, "p2", [128, 512], F32, 3, psum=True)
            pn = Ring(es, nc, S, "p2n", [128, 512], F32, 2, psum=True)
            sqr = Ring(es, nc, S, "n2_sq", [128, 512], F32, 2)
            rr = Ring(es, nc, S, "n2_r", [128, 512], F32, 2)
            stb = Ring(es, nc, S, "stb2", [128, 512], BF16, 4)
            stf = Ring(es, nc, S, "stf2", [128, 512], F32, 3)
            wst = Ring(es, nc, S, "w2st", [128, 8, 512], F32, 2)
            wbr = Ring(es, nc, S, "w2b", [128, 8, 512], BF16, 2)
            wv = W[l]["w_q"].rearrange("(kc p) n -> p kc n", p=128)
            for ct in range(13):
                c0 = ct * 512
                n = 512 if ct < 12 else 384
                w_t, rw = wst.next()
                S.dma("act", lambda e, w_t=w_t, c0=c0, n=n: e.dma_start(out=w_t[:, :, 0:n], in_=wv[:, :, c0:c0 + n]), writes=[rw])
                wb, rwb = wbr.next()
                for kc in range(8):
                    if kc % 2:
                        S.op("act", lambda e, w_t=w_t, wb=wb, kc=kc, n=n: e.activation(out=wb[:, kc, 0:n], in_=w_t[:, kc, 0:n], func=AF.Copy, scale=cst_t[:, kc:kc + 1]), reads=[rw, r_cst], writes=[rwb])
                    else:
                        S.op("dve", lambda e, w_t=w_t, wb=wb, kc=kc, n=n: e.tensor_scalar(out=wb[:, kc, 0:n], in0=w_t[:, kc, 0:n], scalar1=cst_t[:, kc:kc + 1], scalar2=None, op0=ALU.mult), reads=[rw, r_cst], writes=[rwb])
                for cc in range(n // 128):
                    ch = ct * 4 + cc
                    if ch == 50:
                        for blk in range(16):
                            p, rp = ps.next()
                            for kc in range(8):
                                S.op("pe", lambda e, p=p, kc=kc, blk=blk, wb=wb, cc=cc: e.matmul(p[:, 0:8], lhsT=hT[:, kc, blk * 128:(blk + 1) * 128], rhs=wb[:, kc, cc * 128:cc * 128 + 8], start=(kc == 0), stop=(kc == 7)), reads=[rwb, r_hT], writes=[rp])
                            S.op("act", lambda e, p=p, blk=blk: e.activation(out=iw_t[:, blk, :], in_=p[:, 0:8], func=AF.Copy, scale=8 ** -0.5), reads=[rp], writes=[r_iw])
                        continue
                    for tg in range(4):
                        tsl = slice(tg * 512, (tg + 1) * 512)
                        p, rp = ps.next()
                        for kc in range(8):
                            S.op("pe", lambda e, p=p, kc=kc, wb=wb, cc=cc, tsl=tsl: e.matmul(p, lhsT=wb[:, kc, cc * 128:(cc + 1) * 128], rhs=hT[:, kc, tsl], start=(kc == 0), stop=(kc == 7)), reads=[rwb, r_hT], writes=[rp])
                        if ch < 26:
                            so, rso = stb.next()
                            if ch < 2:
                                S.op("act", lambda e, so=so, p=p: e.activation(out=so, in_=p, func=AF.Copy, scale=0.125), reads=[rp], writes=[rso])
                                dst, rd = D_gqT[ch, :, tsl], R_gqT
                            elif ch < 6:
                                fm_norm((sqr, pn, rr), p, rp, 128, 11, 128 ** -0.5, so, rso)
                                dst, rd = D_bqT[ch - 2, :, tsl], R_bqT
                            elif ch < 10:
                                S.op("act", lambda e, so=so, p=p: e.activation(out=so, in_=p, func=AF.Copy, scale=0.125), reads=[rp], writes=[rso])
                                dst, rd = D_iqT[ch - 6, :, tsl], R_iqT
                            elif ch < 14:
                                fm_norm((sqr, pn, rr), p, rp, 64, 10, 0.125, so, rso)
                                dst, rd = D_cqT[ch - 10, :, tsl], R_cqT
                            else:
                                S.op("act", lambda e, so=so, p=p: e.activation(out=so, in_=p, func=AF.Silu), reads=[rp], writes=[rso])
                                dst, rd = D_zT[(ch - 14) // 4, (ch - 14) % 4, :, tsl], R_zT
                            S.dma("pool", lambda e, so=so, dst=dst: e.dma_start(out=dst, in_=so), reads=[rso], writes=[rd])
                        else:
                            sf, rsf = stf.next()
                            S.op("act", lambda e, sf=sf, p=p: e.activation(out=sf, in_=p, func=AF.Sigmoid), reads=[rp], writes=[rsf])
                            S.dma("pool", lambda e, sf=sf, ch=ch, tsl=tsl: e.dma_start(out=D_gT[ch - 26, :, tsl], in_=sf), reads=[rsf], writes=[R_gT])
            S.fence()

        with ExitStack() as es:
            er = epi_rings(es, "g_")
            val_t = sbt(es, "val_t", [128, 128]); r_val = S.res()
            S.dma("sp", lambda e: e.dma_start(out=val_t, in_=cvar["valid"][v]), writes=[r_val])
            qT = sbt(es, "g_qT", [64, 4, 2048], BF16); r_qT = S.res()
            S.dma("sp", lambda e: e.dma_start(out=qT, in_=D_gqT.rearrange("a (b p) t -> p (a b) t", b=2)), reads=[R_gqT], writes=[r_qT])
            St = sbt(es, "g_S", [64, 4, 128]); r_S = S.res()
            Sb = Ring(es, nc, S, "g_Sb", [64, 4, 128], BF16, 2)
            S.op("dve", lambda e: e.memset(St, 0.0), writes=[r_S])
            kr = Ring(es, nc, S, "g_k", [128, 8, 256], BF16, 2)
            vr = Ring(es, nc, S, "g_v", [128, 8, 512], BF16, 2)
            pu = Ring(es, nc, S, "g_pu", [64, 4, 128], F32, 4, psum=True)
            po = Ring(es, nc, S, "g_po", [128, 4, 64], F32, 2, psum=True)
            acc = Ring(es, nc, S, "g_acc", [128, 4, 8, 64], F32, 2)
            kv = D_kdec.rearrange("(b p) c -> p b c", p=128)
            vv = D_gv.rearrange("(b p) c -> p b c", p=128)
            for c in range(128):
                if c % 16 == 0:
                    k_t, rk = kr.next(); v_t, rv = vr.next()
                    b0 = c // 2
                    S.dma("sp", lambda e, k_t=k_t, b0=b0: e.dma_start(out=k_t, in_=kv[:, b0:b0 + 8, :]), reads=[R_kdec], writes=[rk])
                    S.dma("sp", lambda e, v_t=v_t, b0=b0: e.dma_start(out=v_t, in_=vv[:, b0:b0 + 8, :]), reads=[R_gv], writes=[rv])
                if c % 32 == 0:
                    a_t, ra = acc.next()
                    S.op("pool", lambda e, a_t=a_t: e.memset(a_t, 0.0), writes=[ra])
                bi = (c % 16) // 2
                pb = (c % 2) * 64
                u, ru = pu.next()
                for h in range(4):
                    S.op("pe", lambda e, u=u, h=h, k_t=k_t, v_t=v_t, bi=bi, pb=pb: e.matmul(u[:, h, :], lhsT=k_t[pb:pb + 64, bi, h * 64:(h + 1) * 64], rhs=v_t[pb:pb + 64, bi, h * 128:(h + 1) * 128], start=True, stop=True), reads=[rk, rv], writes=[ru])
                S.op("dve", lambda e, c=c: e.tensor_tensor(out=St, in0=St, in1=dec[:, :, c:c + 1].to_broadcast([64, 4, 128]), op=ALU.mult), reads=[r_S, r_dec], writes=[r_S])
                S.op("dve", lambda e, u=u: e.tensor_tensor(out=St, in0=St, in1=u, op=ALU.add), reads=[r_S, ru], writes=[r_S])
                m = c // 32
                lc = c % 8
                if v == 4 or (c - 32 * m) // 8 == v:
                    s_b, rsb = Sb.next()
                    S.op("act", lambda e, s_b=s_b: e.activation(out=s_b, in_=St, func=AF.Copy), reads=[r_S], writes=[rsb])
                    o, ro = po.next()
                    t0 = m * 512 + lc * 64
                    for h in range(4):
                        S.op("pe", lambda e, o=o, h=h, s_b=s_b, t0=t0: e.matmul(o[:, h, :], lhsT=s_b[:, h, :], rhs=qT[:, h, t0:t0 + 64], start=True, stop=True), reads=[rsb, r_qT], writes=[ro])
                    S.op("dve", lambda e, o=o, a_t=a_t, lc=lc, c=c: e.scalar_tensor_tensor(out=a_t[:, :, lc, :], in0=o, scalar=val_t[:, c:c + 1], in1=a_t[:, :, lc, :], op0=ALU.mult, op1=ALU.add), reads=[ro, ra, r_val], writes=[ra])
                if c % 32 == 31:
                    for h in range(4):
                        branch_epilogue(er, 0, h, m, a_t[:, h, :, :].rearrange("p a b -> p (a b)"), ra, True, 12, None)
            S.fence()

        with ExitStack() as es:
            attention(es, v, 2, D_ckT, R_ckT, D_cv, R_cv, D_cqT, R_cqT, 2, False)

        with ExitStack() as es:
            iq = sbt(es, "i_q", [128, 8, 2048], BF16); r_iq = S.res()
            S.op("pool", lambda e: e.memset(iq, 0.0), writes=[r_iq])
            for h_ in range(8):
                hb_ = (h_ % 2) * 64
                S.dma("sp", lambda e, h_=h_, hb_=hb_: e.dma_start(out=iq[hb_:hb_ + 64, h_, :], in_=D_iqT[h_ // 2, hb_:hb_ + 64, :]), reads=[R_iqT], writes=[r_iq])
            io_t = sbt(es, "i_iota", [128, 2048]); r_io = S.res()
            S.dma("sp", lambda e: e.dma_start(out=io_t, in_=iota), writes=[r_io])
            vr_t = sbt(es, "i_vr", [128, 16]); r_vr = S.res()
            S.dma("sp", lambda e: e.dma_start(out=vr_t, in_=cvar["vis_rel"][v]), writes=[r_vr])
            thr_all = sbt(es, "i_thr", [128, 16]); r_thr = S.res()
            cnt_all = sbt(es, "i_cnt", [128, 16]); r_cnt = S.res()
            sc_r = Ring(es, nc, S, "i_sc", [128, 8192], F32, 3)
            nm_r = Ring(es, nc, S, "i_nm", [128, 8192], BF16, 1)
            junk = Ring(es, nc, S, "i_junk", [128, 4608], BF16, 1)
            junka = Ring(es, nc, S, "i_junka", [128, 3840], BF16, 1)
            wtab = Ring(es, nc, S, "i_wtab", [128, K_IT], F32, 2)
            cact = Ring(es, nc, S, "i_cact", [128, 1], F32, 4)
            p2_t = sbt(es, "i_p2", [128, K_IT]); r_p2 = S.res()
            S.dma("sp", lambda e: e.dma_start(out=p2_t, in_=pow2_in), writes=[r_p2])
            pd = Ring(es, nc, S, "i_pd", [128, 512], F32, 4, psum=True)
            pacc = Ring(es, nc, S, "i_pacc", [128, 512], F32, 2, psum=True)
            dgr = Ring(es, nc, S, "i_dg", [128, 8, 128], BF16, 2)
            absw_t = sbt(es, "i_absw", [128, 16, 8]); sgn_t = sbt(es, "i_sgn", [128, 16, 8], BF16); r_sgn = S.res()
            S.op("act", lambda e: e.activation(out=absw_t, in_=iw_t, func=AF.Abs), reads=[r_iw], writes=[r_sgn])
            S.op("act", lambda e: e.activation(out=sgn_t, in_=iw_t, func=AF.Sign), reads=[r_iw, r_sgn], writes=[r_sgn])
            ptr = Ring(es, nc, S, "i_pt", [128, 4, 128], BF16, 2, psum=True)
            rl = Ring(es, nc, S, "i_rl", [128, 512], BF16, 5)
            tro = Ring(es, nc, S, "i_tro", [128, 4, 128], BF16, 5)
            bs = Ring(es, nc, S, "i_bs", [128, 8], F32, 2)
            def s1_gen(qb, st):
                m = qb // 4
                nk = 2048 * (m + 1) if v == 4 else 512 * (4 * m + v + 1)
                nfill = nk - 2048 * m
                sc, rsc = sc_r.next()
                st.update(qb=qb, m=m, nk=nk, nfill=nfill, sc=sc, rsc=rsc)
                dg, rdg = dgr.next()
                S.op("dve", lambda e, dg=dg, qb=qb: e.tensor_tensor(out=dg, in0=ident_b.rearrange("p (o q) -> p o q", o=1).to_broadcast([128, 8, 128]), in1=sgn_t[:, qb, :].rearrange("p (h o) -> p h o", o=1).to_broadcast([128, 8, 128]), op=ALU.mult), reads=[r_cmb, r_sgn], writes=[rdg])
                for kg in range(nk // 512):
                    pa, rpa = pacc.next()
                    pend = []

                    def back(item):
                        h_, r_t_, rr__ = item
                        S.op("pe", lambda e: e.matmul(pa, lhsT=dg[:, h_, :], rhs=r_t_, start=(h_ == 0), stop=(h_ == 7)), reads=[rdg, rr__], writes=[rpa])

                    for h in range(8):
                        d, rd = pd.next()
                        hb = (h % 2) * 64
                        S.op("pe", lambda e, d=d, h=h, hb=hb, qb=qb, kg=kg: e.matmul(d, lhsT=iq[:, h, qb * 128:(qb + 1) * 128], rhs=ikT[:, kg * 512:(kg + 1) * 512], start=True, stop=True), reads=[r_iq, r_ikT], writes=[rd])
                        r_t, rr_ = rl.next()
                        S.op("act", lambda e, d=d, r_t=r_t, qb=qb, h=h: e.activation(out=r_t, in_=d, func=AF.Relu, scale=absw_t[:, qb, h:h + 1]), reads=[rd, r_sgn], writes=[rr_])
                        pend.append((h, r_t, rr_))
                        if len(pend) > 2:
                            back(pend.pop(0))
                    while pend:
                        back(pend.pop(0))
                    dst = sc[:, kg * 512:(kg + 1) * 512]
                    if kg % 2:
                        S.op("act", lambda e, dst=dst, pa=pa: e.activation(out=dst, in_=pa, func=AF.Copy), reads=[rpa], writes=[rsc])
                    else:
                        S.op("dve", lambda e, dst=dst, pa=pa: e.tensor_copy(out=dst, in_=pa), reads=[rpa], writes=[rsc])
                    yield

            def qb_gen(st):
                qb = st["qb"]; m = st["m"]; nk = st["nk"]; nfill = st["nfill"]; sc = st["sc"]; rsc = st["rsc"]
                b, rb_ = bs.next()
                S.op("dve", lambda e, b=b, sc=sc, nk=nk: e.tensor_reduce(out=b[:, 0:1], in_=sc[:, 0:nk], axis=AX.X, op=ALU.min), reads=[rsc], writes=[rb_])
                S.op("dve", lambda e, b=b, sc=sc, nk=nk: e.tensor_reduce(out=b[:, 1:2], in_=sc[:, 0:nk], axis=AX.X, op=ALU.max), reads=[rsc, rb_], writes=[rb_])
                S.op("dve", lambda e, b=b: e.scalar_tensor_tensor(out=b[:, 1:2], in0=b[:, 1:2], scalar=1.0, in1=b[:, 0:1], op0=ALU.add, op1=ALU.subtract), reads=[rb_], writes=[rb_])
                wt, rwt = wtab.next()
                S.op("dve", lambda e, b=b, wt=wt: e.tensor_scalar(out=wt, in0=p2_t, scalar1=b[:, 1:2], scalar2=None, op0=ALU.mult), reads=[rb_, r_p2], writes=[rwt])
                S.op("dve", lambda e, sc=sc, nk=nk, qb=qb, nfill=nfill: e.scalar_tensor_tensor(out=sc[:, nk - nfill:nk], in0=io_t[:, 0:nfill], scalar=vr_t[:, qb:qb + 1], in1=sc[:, nk - nfill:nk], op0=ALU.subtract, op1=ALU.min), reads=[rsc, r_io, r_vr, rb_], writes=[rsc])
                jk, rj = junk.next()
                ja, rja = junka.next()
                nh = (nk * 9 // 16) // 128 * 128
                r_mid = S.res(); r_cd = S.res(); r_u = S.res()
                S.op("dve", lambda e, b=b, wt=wt: e.tensor_tensor(out=b[:, 2:3], in0=b[:, 0:1], in1=wt[:, 0:1], op=ALU.add), reads=[rb_, rwt], writes=[r_mid])
                yield
                for it in range(K_IT):
                    S.op("dve", lambda e, b=b, sc=sc, jk=jk, nh=nh: e.tensor_scalar(out=jk[:, 0:nh], in0=sc[:, 0:nh], scalar1=b[:, 2:3], scalar2=0.0, op0=ALU.is_ge, op1=ALU.add, accum_out=b[:, 3:4]), reads=[rsc, r_mid], writes=[rj, r_cd])
                    ca, rca = cact.next()
                    S.op("act", lambda e, b=b, sc=sc, ja=ja, nh=nh, nk=nk, ca=ca: e.activation(out=ja[:, 0:nk - nh], in_=sc[:, nh:nk], func=AF.Sign, scale=-1.0, bias=b[:, 2:3], accum_out=ca[:, 0:1]), reads=[rsc, r_mid], writes=[rja, rca])
                    S.op("dve", lambda e, b=b, ca=ca: e.scalar_tensor_tensor(out=b[:, 4:5], in0=b[:, 3:4], scalar=2.0, in1=ca[:, 0:1], op0=ALU.mult, op1=ALU.subtract), reads=[r_cd, rca], writes=[r_u])
                    S.op("dve", lambda e, b=b, wt=wt, it=it, nk=nk, nh=nh: e.tensor_scalar(out=b[:, 5:6], in0=b[:, 4:5], scalar1=float(2 * TOPK - 1 - (nk - nh)), scalar2=wt[:, it:it + 1], op0=ALU.is_ge, op1=ALU.mult), reads=[r_u, rwt], writes=[r_u])
                    nx = it + 1 if it < K_IT - 1 else it
                    S.op("dve", lambda e, b=b, wt=wt, nx=nx: e.scalar_tensor_tensor(out=b[:, 2:3], in0=b[:, 2:3], scalar=wt[:, nx:nx + 1], in1=b[:, 5:6], op0=ALU.subtract, op1=ALU.add), reads=[r_mid, r_u, rwt], writes=[r_mid])
                    yield
                S.op("dve", lambda e, b=b: e.tensor_copy(out=b[:, 0:1], in_=b[:, 2:3]), reads=[r_mid, rb_], writes=[rb_])
                if debug:
                    pass
                    S.op("dve", lambda e, b=b, qb=qb: e.tensor_copy(out=thr_all[:, qb:qb + 1], in_=b[:, 0:1]), reads=[rb_], writes=[r_thr])
                nm, rnm = nm_r.next()
                S.op("dve", lambda e, nm=nm, sc=sc, b=b, nk=nk: e.tensor_scalar(out=nm[:, 0:nk], in0=sc[:, 0:nk], scalar1=b[:, 0:1], scalar2=NEG, op0=ALU.is_lt, op1=ALU.mult), reads=[rsc, rb_], writes=[rnm])
                for g in range(nk // 512):
                    pt, rpt = ptr.next()
                    for a in range(4):
                        S.op("pe", lambda e, pt=pt, a=a, g=g, nm=nm: e.transpose(pt[:, a, :], nm[:, (g * 4 + a) * 128:(g * 4 + a + 1) * 128], ident_b), reads=[rnm, r_cmb], writes=[rpt])
                    to, rto = tro.next()
                    S.op("dve", lambda e, to=to, pt=pt: e.tensor_copy(out=to, in_=pt), reads=[rpt], writes=[rto])
                    S.dma("pool", lambda e, to=to, qb=qb, g=g: e.dma_start(out=D_nm[qb // 4, g * 4:g * 4 + 4, :, (qb % 4) * 128:(qb % 4 + 1) * 128].rearrange("a p q -> p a q"), in_=to), reads=[rto], writes=[R_nm[qb]])
                yield
            def run_all(g):
                for _ in g:
                    pass

            sts = [dict() for _ in range(16)]
            run_all(s1_gen(0, sts[0]))
            run_all(s1_gen(1, sts[1]))
            for pair in range(8):
                qa, qb_ = 2 * pair, 2 * pair + 1
                gens = [qb_gen(sts[qa]), qb_gen(sts[qb_])]
                for g in gens:
                    next(g)
                nxt = s1_gen(qa + 2, sts[qa + 2]) if qa + 2 < 16 else None
                for it in range(K_IT):
                    for g in gens:
                        next(g)
                    if nxt is not None:
                        try:
                            next(nxt)
                        except StopIteration:
                            nxt = None
                if nxt is not None:
                    run_all(nxt)
                next(gens[0])
                if qa + 3 < 16:
                    run_all(s1_gen(qa + 3, sts[qa + 3]))
                next(gens[1])
            if debug:
                S.dma("pool", lambda e: e.dma_start(out=dbg["thr"], in_=thr_all), reads=[r_thr], writes=[newout()])
                S.dma("pool", lambda e: e.dma_start(out=dbg["cnt"], in_=cnt_all), reads=[r_cnt], writes=[newout()])
            S.fence()

        with ExitStack() as es:
            attention(es, v, 1, D_bkT, R_bkT, D_bv, R_bv, D_bqT, R_bqT, 1, True)

        with ExitStack() as es:
            wbr_b = sbt(es, "t_wbr", [128, 12, 1024], BF16); r_wbr = S.res()
            wo_b = sbt(es, "t_wo", [128, 8, 1024], BF16); r_wo = S.res()
            with ExitStack() as es2:
                st = Ring(es2, nc, S, "t_wst", [128, 4, 1024], F32, 2)
                for i in range(3):
                    s_t, rs = st.next()
                    S.dma("sp", lambda e, s_t=s_t, i=i: e.dma_start(out=s_t, in_=W[l]["w_br"][i].rearrange("(kc p) n -> p kc n", p=128)), writes=[rs])
                    S.op("dve", lambda e, s_t=s_t, i=i: e.tensor_copy(out=wbr_b[:, i * 4:(i + 1) * 4, :], in_=s_t), reads=[rs], writes=[r_wbr])
                for i in range(2):
                    s_t, rs = st.next()
                    S.dma("sp", lambda e, s_t=s_t, i=i: e.dma_start(out=s_t, in_=W[l]["w_out"][i * 512:(i + 1) * 512].rearrange("(kc p) n -> p kc n", p=128)), writes=[rs])
                    S.op("dve", lambda e, s_t=s_t, i=i: e.tensor_copy(out=wo_b[:, i * 4:(i + 1) * 4, :], in_=s_t), reads=[rs], writes=[r_wo])
                S.fence()
            yT = Ring(es, nc, S, "t_yT", [128, 12, 512], BF16, 2)
            gT = Ring(es, nc, S, "t_gT", [128, 3, 512], F32, 3)
            mT = Ring(es, nc, S, "t_mT", [128, 8, 512], BF16, 2)
            pb = Ring(es, nc, S, "t_pb", [128, 512], F32, 3, psum=True)
            po = Ring(es, nc, S, "t_po", [128, 512], F32, 2, psum=True)
            ot = Ring(es, nc, S, "t_ot", [128, 512], F32, 3)
            xc7 = Ring(es, nc, S, "t_xc", [128, 8, 512], F32, 1)
            accr = Ring(es, nc, S, "t_acc", [128, 512], F32, 2)
            tmp = Ring(es, nc, S, "t_tmp", [128, 512], F32, 2)
            xr = Ring(es, nc, S, "t_x", [128, 8, 512], F32, 1)
            for tg in range(4):
                tsl = slice(tg * 512, (tg + 1) * 512)
                y_t, ry = yT.next()
                S.dma("sp", lambda e, y_t=y_t, tsl=tsl: e.dma_start(out=y_t, in_=D_yT[:, :, :, tsl].rearrange("b k p t -> p (b k) t")), reads=[r for rr_ in R_yT for r in rr_], writes=[ry])
                m_t, rm = mT.next()
                for dc in range(8):
                    g_t, rg = gT.next()
                    S.dma("act", lambda e, g_t=g_t, dc=dc, tsl=tsl: e.dma_start(out=g_t, in_=D_gT.rearrange("(b d) p t -> d p b t", b=3)[dc, :, :, tsl]), reads=[R_gT], writes=[rg])
                    a_t, ra = accr.next()
                    for br in range(3):
                        p, rp = pb.next()
                        for kc in range(4):
                            S.op("pe", lambda e, p=p, br=br, kc=kc, dc=dc, y_t=y_t: e.matmul(p, lhsT=wbr_b[:, br * 4 + kc, dc * 128:(dc + 1) * 128], rhs=y_t[:, br * 4 + kc, :], start=(kc == 0), stop=(kc == 3)), reads=[r_wbr, ry], writes=[rp])
                        if br == 0:
                            S.op("dve", lambda e, a_t=a_t, p=p, g_t=g_t: e.tensor_tensor(out=a_t, in0=p, in1=g_t[:, 0, :], op=ALU.mult), reads=[rp, rg], writes=[ra])
                        else:
                            t_t, rt = tmp.next()
                            S.op("dve", lambda e, t_t=t_t, p=p, g_t=g_t, br=br: e.tensor_tensor(out=t_t, in0=p, in1=g_t[:, br, :], op=ALU.mult), reads=[rp, rg], writes=[rt])
                            if br == 1:
                                S.op("pool", lambda e, a_t=a_t, t_t=t_t: e.tensor_tensor(out=a_t, in0=a_t, in1=t_t, op=ALU.add), reads=[ra, rt], writes=[ra])
                            else:
                                S.op("pool", lambda e, a_t=a_t, t_t=t_t, m_t=m_t, dc=dc: e.tensor_tensor(out=m_t[:, dc, :], in0=a_t, in1=t_t, op=ALU.add), reads=[ra, rt], writes=[rm])
                x_t, rx = xr.next()
                xloader(x_t, rx, tg, xc7)
                for dc2 in range(8):
                    o, ro = po.next()
                    for dc in range(8):
                        S.op("pe", lambda e, o=o, dc2=dc2, dc=dc, m_t=m_t: e.matmul(o, lhsT=wo_b[:, dc, dc2 * 128:(dc2 + 1) * 128], rhs=m_t[:, dc, :], start=(dc == 0), stop=(dc == 7)), reads=[rm, r_wo], writes=[ro])
                    o_t, rot = ot.next()
                    S.op("dve", lambda e, o_t=o_t, x_t=x_t, o=o, dc2=dc2: e.tensor_tensor(out=o_t, in0=x_t[:, dc2, :], in1=o, op=ALU.add), reads=[rx, ro], writes=[rot])
                    writer(o_t, rot, dc2, tg)
            S.fence()

    xvf = xT_full.rearrange("(kc p) t -> p kc t", p=128)
    x1v = D_x1T.rearrange("(kc p) t -> p kc t", p=128)

    def full_loader(xv, R_src):
        def f(x_t, rx, tg, rings=None):
            S.dma("sp", lambda e: e.dma_start(out=x_t, in_=xv[:, :, tg * 512:(tg + 1) * 512]), reads=[R_src] if R_src else [], writes=[rx])
        return f

    def pass_loader(xv, R_src, p):
        def f(x_t, rx, tg, rings=None):
            sb_ = 4 * tg + p
            S.dma("sp", lambda e: e.dma_start(out=x_t, in_=xv[:, :, sb_ * 512:(sb_ + 1) * 512]), reads=[R_src] if R_src else [], writes=[rx])
        return f

    def blend_loader(xv, R_src):
        def f(x_t, rx, tg, rings):
            for p in range(4):
                sb_ = 4 * tg + p
                if p == 0:
                    c_t, rc = x_t, rx
                else:
                    c_t, rc = rings.next()
                S.dma("sp", lambda e: e.dma_start(out=c_t, in_=xv[:, :, sb_ * 512:(sb_ + 1) * 512]), reads=[R_src] if R_src else [], writes=[rc])
                if p == 0:
                    S.op("dve", lambda e: e.tensor_scalar(out=x_t, in0=x_t, scalar1=sel_t[:, 0:1], scalar2=None, op0=ALU.mult), reads=[rx, r_sel], writes=[rx])
                else:
                    S.op("dve", lambda e: e.scalar_tensor_tensor(out=x_t, in0=c_t, scalar=sel_t[:, p:p + 1], in1=x_t, op0=ALU.mult, op1=ALU.add), reads=[rc, rx, r_sel], writes=[rx])
        return f

    def x1_writer(p):
        def f(o_t, ro, dc2, tg):
            sb_ = 4 * tg + p
            S.dma("pool", lambda e: e.dma_start(out=D_x1T[dc2 * 128:(dc2 + 1) * 128, sb_ * 512:(sb_ + 1) * 512], in_=o_t), reads=[ro], writes=[R_x1T])
        return f

    def out_writer(o_t, ro, dc2, tg):
        S.dma("pool", lambda e: e.dma_start(out=outT[dc2 * 128:(dc2 + 1) * 128, tg * 512:(tg + 1) * 512], in_=o_t), reads=[ro], writes=[newout()])

    if fused:
        la, lb = layers
        layer_setup(la)
        kside(la, full_loader(xvf, None))
        for p in range(4):
            rest(la, p, pass_loader(xvf, None, p), x1_writer(p))
        layer_setup(lb)
        kside(lb, full_loader(x1v, R_x1T))
        rest(lb, 4, blend_loader(x1v, R_x1T), out_writer)
    else:
        la = layers[0]
        layer_setup(la)
        kside(la, full_loader(xvf, None))
        rest(la, 4, blend_loader(xvf, None), out_writer)
    S.wait_all("sp", outs)
    S.fence()

    n = S.finalize()
    top.close()
    return n


def _tok_idx(j):
    return np.concatenate([np.arange((4 * m + j) * 512, (4 * m + j + 1) * 512) for m in range(4)])


def _var_consts(j):
    own = _tok_idx(j)
    vis_end = (own // 64 + 1) * 64
    vis_rel = np.zeros((128, 16), np.float32)
    for qb in range(16):
        m = qb // 4
        vis_rel[:, qb] = (vis_end[qb * 128:(qb + 1) * 128] - 2048 * m - 0.5) * (-1e30)
    visA = np.zeros((128, 4, 512), np.float32)
    for m in range(4):
        visA[:, m, :] = vis_end[m * 512:(m + 1) * 512][None, :] - 2048 * m - np.arange(128)[:, None]
    valid = np.zeros((128, 128), np.float32)
    for c in range(128):
        m = c // 32
        if (c - 32 * m) // 8 == j:
            valid[:, c] = 1.0
    return vis_rel, visA, valid


def _const_inputs(j):
    cm = np.zeros((128, 4, 128), np.float32)
    cm[:, 0, :] = 1.0
    cm[:64, 1, :64] = 1.0
    cm[64:, 1, 64:] = 1.0
    for s in range(128):
        for t in range(128):
            if s // 64 == t // 64 and s > t:
                cm[s, 2, t] = 1.0
    cm[:, 3, :] = np.eye(128, dtype=np.float32)
    ind = np.zeros((128, 2), np.float32)
    ind[:64, 0] = 1
    ind[64:, 1] = 1
    iota = np.tile((np.arange(2048, dtype=np.float64) * (-1e30)).astype(np.float32)[None, :], (128, 1))
    pow2 = np.tile((2.0 ** -(np.arange(K_IT) + 1.0)).astype(np.float32)[None, :], (128, 1))
    vs = [_var_consts(p) for p in (0, 1, 2, 3, j)]
    sel = np.zeros((128, 4), np.float32)
    sel[:, j] = 1.0
    return {"cmat": cm, "ind": ind, "iota": iota, "vis_rel": np.stack([v[0] for v in vs]),
            "visA": np.stack([v[1] for v in vs]), "valid": np.stack([v[2] for v in vs]), "sel": sel, "pow2": pow2}


def _layer_weights(P, l):
    lam_init = 0.8 - 0.6 * float(np.exp(-0.3 * l))
    w = P["w_in"][l]
    cst = np.zeros((128, 16), np.float32)
    cst[:, 0:8] = P["norm_g"][l].reshape(8, 128).T
    cst[:, 8] = np.tile(P["diff_kn_g"][l], 2)
    cst[:, 9] = P["dsa_kn_g"][l]
    cst[:, 10] = np.tile(P["diff_qn_g"][l], 2)
    cst[:, 11] = P["dsa_qn_g"][l]
    cst[:, 12] = P["gla_norm_g"][l]
    cst[:, 13] = lam_init
    lqk = np.tile(np.concatenate([P["diff_lq1"][l], P["diff_lk1"][l], P["diff_lq2"][l], P["diff_lk2"][l]])[None, :], (128, 1))
    d = {f"w_k{l}": w[:, wk_cols()], f"w_q{l}": w[:, wq_cols()], f"w_br{l}": P["w_br"][l], f"w_out{l}": P["w_out"][l],
         f"cst{l}": cst, f"wa2{l}": P["gla_wa2"][l], f"ba_bc{l}": np.tile(P["gla_ba"][l][None, :], (128, 1)), f"lqk{l}": lqk}
    return {k: np.ascontiguousarray(v, dtype=np.float32) for k, v in d.items()}


def core_inputs(xb, j, P, layers=(0, 1)):
    d = {"xT_full": np.ascontiguousarray(xb.T, dtype=np.float32)}
    for l in layers:
        d.update(_layer_weights(P, l))
    d.update({k: np.ascontiguousarray(v, dtype=np.float32) for k, v in _const_inputs(j).items()})
    return d


_NC = {}


def _get_nc(layers, fused):
    key = (tuple(layers), fused)
    if key not in _NC:
        nc = bass.Bass("TRN2", target_bir_lowering=False)
        build_fused(nc, layers=layers, fused=fused)
        _NC[key] = nc
    return _NC[key]


def kernel(**inputs):
    P = {k: np.asarray(v, dtype=np.float32) for k, v in inputs.items()}
    x = P["x"]
    nc = _get_nc((0, 1), True)
    wts = {}
    for l in (0, 1):
        wts.update(_layer_weights(P, l))
    in_maps = []
    for c in range(8):
        d = {"xT_full": np.ascontiguousarray(x[c // 4].T, dtype=np.float32)}
        d.update(wts)
        d.update({k: np.ascontiguousarray(v, dtype=np.float32) for k, v in _const_inputs(c % 4).items()})
        in_maps.append(d)
    res = run_bass_kernel_spmd(nc, in_maps, core_ids=list(range(8)))
    out = np.empty_like(x)
    for c in range(8):
        out[c // 4][_tok_idx(c % 4)] = res.results[c]["outT"].T
    return out
```

```python
import numpy as np
from contextlib import ExitStack
import concourse.bass as bass
import concourse.mybir as mybir
from concourse.bass_utils import run_bass_kernel_spmd

F32 = mybir.dt.float32
BF16 = mybir.dt.bfloat16
AF = mybir.ActivationFunctionType
ALU = mybir.AluOpType
AX = mybir.AxisListType

NDS = 16
K_IT = 20
TOPK = 256.0
NEG = -30000.0


import types


def _freeze(fn):
    if fn is None or fn.__closure__ is None:
        return fn
    cells = []
    for c in fn.__closure__:
        try:
            cells.append(types.CellType(c.cell_contents))
        except ValueError:
            cells.append(c)
    g = types.FunctionType(fn.__code__, fn.__globals__, fn.__name__, fn.__defaults__, tuple(cells))
    g.__kwdefaults__ = fn.__kwdefaults__
    return g


class Res:
    __slots__ = ("name", "w", "r")

    def __init__(self, name):
        self.name = name
        self.w = None
        self.r = {}


class Sched:
    def __init__(self, nc):
        self.nc = nc
        self.ops = []
        self.cnt = {k: 0 for k in ("pe", "act", "dve", "pool")}
        self.last = {}
        self.dslot = {q: 0 for q in ("sp", "act", "pool")}
        self.dcnt = {q: [0] * NDS for q in ("sp", "act", "pool")}
        self.dlast = {q: [None] * NDS for q in ("sp", "act", "pool")}
        self.nres = 0

    def res(self, name=None):
        self.nres += 1
        return Res(name or f"r{self.nres}")

    def _deps(self, engine, reads, writes, is_dma):
        deps = []
        for r in reads:
            if r.w is not None:
                deps.append(r.w)
        for w in writes:
            if w.w is not None and (is_dma or engine != "pe" or w.w[0] != ("c", engine)):
                deps.append(w.w)
            for e, t in w.r.items():
                if is_dma or engine != "pe" or e != engine:
                    deps.append(t)
        return deps

    def op(self, engine, fn, reads=(), writes=()):
        deps = self._deps(engine, reads, writes, False)
        self.cnt[engine] += 1
        tok = (("c", engine), self.cnt[engine])
        self.ops.append([engine, _freeze(fn), deps, tok, False])
        for r in reads:
            r.r[engine] = tok
        for w in writes:
            w.w = tok
            w.r = {}
        self.last[("c", engine)] = tok
        return tok

    def dma(self, q, fn, reads=(), writes=()):
        deps = self._deps(q, reads, writes, True)
        slot = self.dslot[q]
        self.dslot[q] = (slot + 1) % NDS
        if self.dlast[q][slot] is not None:
            deps.append(self.dlast[q][slot])
        self.dcnt[q][slot] += 16
        tok = (("d", q, slot), self.dcnt[q][slot])
        self.dlast[q][slot] = tok
        self.ops.append([q, _freeze(fn), deps, tok, True])
        for r in reads:
            r.r[("dma", q, slot)] = tok
        for w in writes:
            w.w = tok
            w.r = {}
        self.last[("d", q, slot)] = tok
        return tok

    def fence(self):
        deps = list(self.last.values())
        for e in ("pe", "act", "dve", "pool", "sp"):
            self.ops.append([e, None, list(deps), None, False])

    def wait_all(self, engine, resources):
        deps = [r.w for r in resources if r.w is not None]
        self.ops.append([engine, None, deps, None, False])

    def finalize(self):
        nc = self.nc
        engs = {"pe": nc.tensor, "act": nc.scalar, "dve": nc.vector, "pool": nc.gpsimd, "sp": nc.sync}
        needed = set()
        for engine, fn, deps, tok, is_dma in self.ops:
            for d in deps:
                if d[0][0] == "c":
                    needed.add(d)
        newval = {}
        run = {k: 0 for k in self.cnt}
        for engine, fn, deps, tok, is_dma in self.ops:
            if tok is not None and tok[0][0] == "c":
                if tok in needed:
                    run[tok[0][1]] += 1
                newval[tok] = run[tok[0][1]]
        sems = {}

        def sem_of(key):
            if key not in sems:
                sems[key] = nc.alloc_semaphore(name="s_" + "_".join(str(x) for x in key))
            return sems[key]

        seen = {k: {} for k in engs}
        for engine, fn, deps, tok, is_dma in self.ops:
            e = engs[engine]
            mx = {}
            for d in deps:
                key, val = d
                if key[0] == "c":
                    val = newval[d]
                if val > mx.get(key, 0):
                    mx[key] = val
            for key, val in mx.items():
                if seen[engine].get(key, 0) >= val:
                    continue
                e.wait_ge(sem_of(key), val)
                seen[engine][key] = val
            if fn is None:
                continue
            ins = fn(e)
            if is_dma:
                ins.then_inc(sem_of(tok[0]), 16)
            elif tok in needed:
                ins.then_inc(sem_of(tok[0]), 1)
        return len(self.ops)


_UID = [0]


class Ring:
    def __init__(self, es, nc, S, name, shape, dtype, n, psum=False):
        self.items = []
        _UID[0] += 1
        name = f"{name}_u{_UID[0]}_"
        for i in range(n):
            if psum:
                t = es.enter_context(nc.psum_tensor(f"{name}{i}", shape, dtype)).ap()
            else:
                t = es.enter_context(nc.sbuf_tensor(f"{name}{i}", shape, dtype)).ap()
            self.items.append((t, S.res(f"{name}{i}")))
        self.i = 0

    def next(self):
        it = self.items[self.i % len(self.items)]
        self.i += 1
        return it


_NAMES = ["gq", "gk", "gv", "ga", "gz", "bq", "bk", "bv", "iq", "ik", "iw", "bz", "cq", "ck", "cv", "cz", "gate"]
_SIZES = [256, 256, 512, 16, 512, 512, 512, 512, 512, 64, 8, 512, 512, 512, 512, 512, 3072]
_OFF = {}
_o = 0
for _n, _s in zip(_NAMES, _SIZES):
    _OFF[_n] = _o
    _o += _s


def _rng(n, a=0, b=None):
    i = _NAMES.index(n)
    b = _SIZES[i] if b is None else b
    return list(range(_OFF[n] + a, _OFF[n] + b))


def wk_cols():
    c = _rng("ck") + _rng("bk") + _rng("ik") + _rng("ik") + _rng("ga") + [0] * 112
    c += _rng("gk") + _rng("gv") + _rng("bv") + _rng("cv")
    return np.array(c, dtype=np.int64)


NKF = 10 * 128
NK = NKF + 1792


def wq_cols():
    c = _rng("gq") + _rng("bq") + _rng("iq") + _rng("cq") + _rng("gz") + _rng("bz") + _rng("cz") + _rng("gate")
    c += _rng("iw") + [0] * 120
    return np.array(c, dtype=np.int64)


NQF = 256 + 512 * 6 + 3072
NQ = NQF + 128


def build_fused(nc, layers=(0, 1), fused=True, debug=False):
    S = Sched(nc)
    di = lambda n, s, d=F32: nc.dram_tensor(n, s, d, kind="ExternalInput").ap()
    dsc = lambda n, s, d=BF16: nc.dram_tensor(n, s, d, kind="Internal").ap()
    xT_full = di("xT_full", [1024, 8192])
    W = {}
    for l in layers:
        W[l] = {"w_k": di(f"w_k{l}", [1024, NK]), "w_q": di(f"w_q{l}", [1024, NQ]), "w_br": di(f"w_br{l}", [3, 512, 1024]),
                "w_out": di(f"w_out{l}", [1024, 1024]), "cst": di(f"cst{l}", [128, 16]), "wa2": di(f"wa2{l}", [16, 256]),
                "ba_bc": di(f"ba_bc{l}", [128, 256]), "lqk": di(f"lqk{l}", [128, 256])}
    cmat = di("cmat", [128, 4, 128])
    ind = di("ind", [128, 2])
    iota = di("iota", [128, 2048])
    NV = 5
    cvar = {"vis_rel": di("vis_rel", [NV, 128, 16]), "visA": di("visA", [NV, 128, 4, 512]), "valid": di("valid", [NV, 128, 128])}
    sel_in = di("sel", [128, 4])
    pow2_in = di("pow2", [128, K_IT])
    outT = nc.dram_tensor("outT", [1024, 2048], F32, kind="ExternalOutput").ap()
    D_x1T = dsc("D_x1T", [1024, 8192], F32); R_x1T = S.res()
    dbg = {}

    D_ckT = dsc("D_ckT", [4, 128, 8192]); D_bkT = dsc("D_bkT", [4, 128, 8192])
    D_cv = dsc("D_cv", [8192, 512]); D_bv = dsc("D_bv", [8192, 512])
    D_kdec = dsc("D_kdec", [8192, 256]); D_gv = dsc("D_gv", [8192, 512])
    D_cqT = dsc("D_cqT", [4, 128, 2048]); D_bqT = dsc("D_bqT", [4, 128, 2048])
    D_iqT = dsc("D_iqT", [4, 128, 2048]); D_gqT = dsc("D_gqT", [2, 128, 2048])
    D_zT = dsc("D_zT", [3, 4, 128, 2048]); D_gT = dsc("D_gT", [24, 128, 2048], F32)
    D_yT = dbg["yT"] if debug else dsc("D_yT", [3, 4, 128, 2048])
    D_nm = dsc("D_nm", [4, 64, 128, 512])
    R_ckT = [S.res() for _ in range(4)]; R_bkT = [S.res() for _ in range(4)]
    R_cv = S.res(); R_bv = S.res(); R_kdec = S.res(); R_gv = S.res()
    R_cqT = S.res(); R_bqT = S.res(); R_iqT = S.res(); R_gqT = S.res(); R_zT = S.res(); R_gT = S.res()
    R_yT = [[S.res() for _ in range(4)] for _ in range(3)]
    R_nm = [S.res() for _ in range(16)]
    outs = []

    def newout():
        r = S.res(); outs.append(r); return r

    top = ExitStack()
    def sbt(es, n, s, d=F32):
        _UID[0] += 1
        return es.enter_context(nc.sbuf_tensor(f"{n}_u{_UID[0]}", s, d)).ap()

    cst_t = sbt(top, "cst_t", [128, 16]); r_cst = S.res()
    cm = sbt(top, "cm", [128, 4, 128]); r_cm = S.res()
    cmb = sbt(top, "cmb", [128, 4, 128], BF16); r_cmb = S.res()
    ind_t = sbt(top, "ind_t", [128, 2]); r_ind = S.res()
    ikT = sbt(top, "ikT", [128, 8192], BF16); r_ikT = S.res()
    dec = sbt(top, "dec", [64, 4, 128]); r_dec = S.res()
    iw_t = sbt(top, "iw_t", [128, 16, 8]); r_iw = S.res()
    lam = sbt(top, "lam", [128, 4]); r_lam = S.res()
    S.dma("sp", lambda e: e.dma_start(out=cm, in_=cmat), writes=[r_cm])
    S.dma("sp", lambda e: e.dma_start(out=ind_t, in_=ind), writes=[r_ind])
    S.op("dve", lambda e: e.tensor_copy(out=cmb, in_=cm), reads=[r_cm], writes=[r_cmb])
    ones_f = cm[:, 0, :]; bones_f = cm[:, 1, :]; mdec_f = cm[:, 2, :]
    ones_b = cmb[:, 0, :]; ident_b = cmb[:, 3, :]
    eps_t = sbt(top, "eps_t", [128, 4]); r_eps = S.res()
    S.op("dve", lambda e: e.memset(eps_t[:, 0:1], 1e-6), writes=[r_eps])
    S.op("dve", lambda e: e.memset(eps_t[:, 1:2], 1e-6 * 64.0), writes=[r_eps])
    S.op("dve", lambda e: e.memset(eps_t[:, 2:3], 1e-6 * 128.0), writes=[r_eps])
    ebias_t = eps_t[:, 3:4]
    S.op("dve", lambda e: e.memset(eps_t[:, 3:4], -8.0), writes=[r_eps])
    sel_t = sbt(top, "sel_t", [128, 4]); r_sel = S.res()
    S.dma("sp", lambda e: e.dma_start(out=sel_t, in_=sel_in), writes=[r_sel])

    def layer_setup(l):
        S.dma("sp", lambda e: e.dma_start(out=cst_t, in_=W[l]["cst"]), writes=[r_cst])

        with ExitStack() as es:
            lq = sbt(es, "lq", [128, 256]); r_lq = S.res()
            pr = sbt(es, "lpr", [128, 128]); r_pr = S.res()
            sm = sbt(es, "lsm", [128, 2]); r_sm = S.res()
            S.dma("sp", lambda e: e.dma_start(out=lq, in_=W[l]["lqk"]), writes=[r_lq])
            lqv = lq.rearrange("p (a b d) -> p a b d", a=2, b=2)
            S.op("dve", lambda e: e.tensor_tensor(out=pr.rearrange("p (a d) -> p a d", a=2), in0=lqv[:, :, 0, :], in1=lqv[:, :, 1, :], op=ALU.mult), reads=[r_lq], writes=[r_pr])
            S.op("dve", lambda e: e.tensor_reduce(out=sm, in_=pr.rearrange("p (a d) -> p a d", a=2), axis=AX.X, op=ALU.add), reads=[r_pr], writes=[r_sm])
            S.op("act", lambda e: e.activation(out=sm, in_=sm, func=AF.Exp), reads=[r_sm], writes=[r_sm])
            S.op("dve", lambda e: e.tensor_tensor(out=lam[:, 0:1], in0=sm[:, 1:2], in1=sm[:, 0:1], op=ALU.subtract), reads=[r_sm], writes=[r_lam])
            S.op("dve", lambda e: e.tensor_tensor(out=lam[:, 0:1], in0=lam[:, 0:1], in1=cst_t[:, 13:14], op=ALU.subtract), reads=[r_lam, r_cst], writes=[r_lam])
            S.op("dve", lambda e: e.tensor_scalar(out=lam[:, 1:2], in0=cst_t[:, 13:14], scalar1=-1.0, scalar2=1.0, op0=ALU.mult, op1=ALU.add), reads=[r_cst, r_lam], writes=[r_lam])
            S.fence()

    def make_h(es, xloader, ntg, hT, r_hT, multi=False):
        xs = Ring(es, nc, S, "xs", [128, 8, 512], F32, 2)
        xc = Ring(es, nc, S, "xc", [128, 8, 512], F32, 1)
        sq = Ring(es, nc, S, "sq", [128, 8, 512], F32, 1)
        pss = Ring(es, nc, S, "pss", [128, 512], F32, 2, psum=True)
        rb = Ring(es, nc, S, "rb", [128, 512], F32, 2)
        for tg in range(ntg):
            x_t, rx = xs.next()
            xloader(x_t, rx, tg, xc)
            s_t, rsq = sq.next()
            S.op("act", lambda e, x_t=x_t, s_t=s_t: e.activation(out=s_t, in_=x_t, func=AF.Square), reads=[rx], writes=[rsq])
            p, rp = pss.next()
            for kc in range(8):
                S.op("pe", lambda e, p=p, s_t=s_t, kc=kc: e.matmul(p, lhsT=ones_f, rhs=s_t[:, kc, :], start=(kc == 0), stop=(kc == 7)), reads=[rsq, r_cm], writes=[rp])
            r_t, rr = rb.next()
            S.op("act", lambda e, p=p, r_t=r_t: e.activation(out=r_t, in_=p, func=AF.Ln, scale=1.0 / 1024, bias=eps_t[:, 0:1]), reads=[rp, r_eps], writes=[rr])
            S.op("act", lambda e, r_t=r_t: e.activation(out=r_t, in_=r_t, func=AF.Exp, scale=-0.5), reads=[rr], writes=[rr])
            g = tg if multi else 0
            S.op("dve", lambda e, x_t=x_t, r_t=r_t, g=g: e.tensor_tensor(out=hT[:, :, g * 512:(g + 1) * 512], in0=x_t, in1=r_t.rearrange("p (o t) -> p o t", o=1).to_broadcast([128, 8, 512]), op=ALU.mult), reads=[rx, rr], writes=[r_hT])
            yield tg

    def load_w(es, wsrc, c0, n, name, r_w):
        wb = sbt(es, name, [128, 8, n], BF16)
        wv = wsrc.rearrange("(kc p) n -> p kc n", p=128)
        with ExitStack() as es2:
            ws = Ring(es2, nc, S, name + "_st", [128, 8, 512], F32, 2)
            for a in range(0, n, 512):
                m = min(512, n - a)
                w_t, rw = ws.next()
                S.dma("act", lambda e, w_t=w_t, a=a, m=m: e.dma_start(out=w_t[:, :, 0:m], in_=wv[:, :, c0 + a:c0 + a + m]), writes=[rw])
                for kc in range(8):
                    if kc % 2:
                        S.op("act", lambda e, w_t=w_t, kc=kc, a=a, m=m: e.activation(out=wb[:, kc, a:a + m], in_=w_t[:, kc, 0:m], func=AF.Copy, scale=cst_t[:, kc:kc + 1]), reads=[rw, r_cst], writes=[r_w])
                    else:
                        S.op("dve", lambda e, w_t=w_t, kc=kc, a=a, m=m: e.tensor_scalar(out=wb[:, kc, a:a + m], in0=w_t[:, kc, 0:m], scalar1=cst_t[:, kc:kc + 1], scalar2=None, op0=ALU.mult), reads=[rw, r_cst], writes=[r_w])
            S.fence()
        return wb

    def fm_norm(rings, p, rp, hd, gidx, scale, dst, rdst):
        sqr, pn, rr = rings
        s_t, rsq = sqr.next()
        S.op("act", lambda e: e.activation(out=s_t, in_=p, func=AF.Square), reads=[rp], writes=[rsq])
        p2, rp2 = pn.next()
        mat = ones_f if hd == 128 else bones_f
        S.op("pe", lambda e: e.matmul(p2, lhsT=mat, rhs=s_t, start=True, stop=True), reads=[rsq, r_cm], writes=[rp2])
        r_t, rrr = rr.next()
        ei = {0.125: 1, 128 ** -0.5: 2}.get(scale, 0)
        S.op("act", lambda e: e.activation(out=r_t, in_=p2, func=AF.Ln, scale=1.0 / (hd * scale * scale), bias=eps_t[:, ei:ei + 1]), reads=[rp2, r_eps], writes=[rrr])
        S.op("act", lambda e: e.activation(out=r_t, in_=r_t, func=AF.Exp, scale=-0.5), reads=[rrr], writes=[rrr])
        if gidx is None:
            S.op("dve", lambda e: e.tensor_tensor(out=dst, in0=p, in1=r_t, op=ALU.mult), reads=[rp, rrr], writes=[rdst])
        else:
            S.op("dve", lambda e: e.scalar_tensor_tensor(out=dst, in0=p, scalar=cst_t[:, gidx:gidx + 1], in1=r_t, op0=ALU.mult, op1=ALU.mult), reads=[rp, rrr, r_cst], writes=[rdst])

    def kside(l, xloader):
        with ExitStack() as es:
            r_wk = S.res()
            wkb = load_w(es, W[l]["w_k"], 0, NK, "wkb", r_wk)
            wa2_t = sbt(es, "wa2_t", [16, 256]); r_wa2 = S.res()
            ba_t = sbt(es, "ba_t", [128, 256]); r_ba = S.res()
            S.dma("sp", lambda e: e.dma_start(out=wa2_t, in_=W[l]["wa2"]), writes=[r_wa2])
            S.dma("sp", lambda e: e.dma_start(out=ba_t, in_=W[l]["ba_bc"]), writes=[r_ba])
            hT = sbt(es, "hT1", [128, 8, 512], BF16); r_hT = S.res()
            ps = Ring(es, nc, S, "p1", [128, 512], F32, 4, psum=True)
            pn = Ring(es, nc, S, "p1n", [128, 512], F32, 2, psum=True)
            sqr = Ring(es, nc, S, "n_sq", [128, 512], F32, 2)
            rr = Ring(es, nc, S, "n_r", [128, 512], F32, 2)
            stb = Ring(es, nc, S, "stb1", [128, 512], BF16, 6)
            gaT = Ring(es, nc, S, "gaT", [16, 512], F32, 2)
            tz = Ring(es, nc, S, "tz", [128, 256], F32, 2)
            tl = Ring(es, nc, S, "tl", [128, 256], F32, 2)
            te = Ring(es, nc, S, "te", [128, 256], F32, 2)
            dsm = Ring(es, nc, S, "dsm", [128, 4], F32, 2)
            for tg in make_h(es, xloader, 16, hT, r_hT):
                tsl = slice(tg * 512, (tg + 1) * 512)
                for ch in range(10):
                    p, rp = ps.next()
                    for kc in range(8):
                        S.op("pe", lambda e, p=p, kc=kc, ch=ch: e.matmul(p, lhsT=wkb[:, kc, ch * 128:(ch + 1) * 128], rhs=hT[:, kc, :], start=(kc == 0), stop=(kc == 7)), reads=[r_wk, r_hT], writes=[rp])
                    if ch < 8:
                        so, rso = stb.next()
                        if ch < 4:
                            fm_norm((sqr, pn, rr), p, rp, 64, 8, 1.0, so, rso)
                            S.dma("pool", lambda e, so=so, ch=ch, tsl=tsl: e.dma_start(out=D_ckT[ch, :, tsl], in_=so), reads=[rso], writes=[R_ckT[ch]])
                        else:
                            fm_norm((sqr, pn, rr), p, rp, 128, 9, 1.0, so, rso)
                            S.dma("pool", lambda e, so=so, ch=ch, tsl=tsl: e.dma_start(out=D_bkT[ch - 4, :, tsl], in_=so), reads=[rso], writes=[R_bkT[ch - 4]])
                    elif ch == 8:
                        S.op("act", lambda e, p=p, tsl=tsl: e.activation(out=ikT[:, tsl], in_=p, func=AF.Copy), reads=[rp], writes=[r_ikT])
                    else:
                        g_t, rg = gaT.next()
                        S.op("act", lambda e, p=p, g_t=g_t: e.activation(out=g_t, in_=p[0:16, :], func=AF.Copy), reads=[rp], writes=[rg])
                for bl in range(4):
                    blk = tg * 4 + bl
                    rows = slice(blk * 128, (blk + 1) * 128)
                    lt = hT[:, :, bl * 128:(bl + 1) * 128]
                    p2, rp2 = pn.next()
                    S.op("pe", lambda e, p2=p2, g_t=g_t, bl=bl: e.matmul(p2[:, 0:256], lhsT=g_t[:, bl * 128:(bl + 1) * 128], rhs=wa2_t, start=True, stop=True), reads=[rg, r_wa2], writes=[rp2])
                    z_t, rz = tz.next()
                    S.op("dve", lambda e, z_t=z_t, p2=p2: e.tensor_tensor(out=z_t, in0=p2[:, 0:256], in1=ba_t, op=ALU.add), reads=[rp2, r_ba], writes=[rz])
                    e_t, re_ = te.next()
                    S.op("act", lambda e, z_t=z_t, e_t=e_t: e.activation(out=e_t, in_=z_t, func=AF.Exp, scale=-1.0), reads=[rz], writes=[re_])
                    l_t, rl = tl.next()
                    S.op("act", lambda e, e_t=e_t, l_t=l_t: e.activation(out=l_t, in_=e_t, func=AF.Ln, bias=1.0), reads=[re_], writes=[rl])
                    for ti, (c0, n) in enumerate(((0, 256), (256, 512), (768, 512), (1280, 512))):
                        p, rp = ps.next()
                        for kc in range(8):
                            S.op("pe", lambda e, p=p, kc=kc, lt=lt, c0=c0, n=n: e.matmul(p[:, 0:n], lhsT=lt[:, kc, :], rhs=wkb[:, kc, NKF + c0:NKF + c0 + n], start=(kc == 0), stop=(kc == 7)), reads=[r_wk, r_hT], writes=[rp])
                        so, rso = stb.next()
                        if ti == 0:
                            p3, rp3 = pn.next()
                            S.op("pe", lambda e, p3=p3, l_t=l_t: e.matmul(p3[:, 0:256], lhsT=mdec_f, rhs=l_t, start=True, stop=True), reads=[r_cm, rl], writes=[rp3])
                            e2, re2 = te.next()
                            S.op("act", lambda e, e2=e2, p3=p3: e.activation(out=e2, in_=p3[:, 0:256], func=AF.Exp, scale=-1.0 / 16), reads=[rp3], writes=[re2])
                            S.op("dve", lambda e, so=so, p=p, e2=e2: e.tensor_tensor(out=so[:, 0:256], in0=p[:, 0:256], in1=e2, op=ALU.mult), reads=[rp, re2], writes=[rso])
                            S.dma("pool", lambda e, so=so, rows=rows: e.dma_start(out=D_kdec[rows, :], in_=so[:, 0:256]), reads=[rso], writes=[R_kdec])
                            p4, rp4 = pn.next()
                            for pr_ in range(4):
                                S.op("pe", lambda e, p4=p4, l_t=l_t, pr_=pr_: e.matmul(p4[0:64, pr_ * 2:pr_ * 2 + 2], lhsT=l_t[:, pr_ * 64:(pr_ + 1) * 64], rhs=ind_t, start=True, stop=True), reads=[rl, r_ind], writes=[rp4])
                            S.op("act", lambda e, p4=p4, blk=blk: e.activation(out=dec[:, :, blk * 2:blk * 2 + 2], in_=p4[0:64, 0:8].rearrange("p (a c) -> p a c", a=4), func=AF.Exp, scale=-1.0 / 16), reads=[rp4], writes=[r_dec])
                        else:
                            S.op("act", lambda e, so=so, p=p: e.activation(out=so, in_=p, func=AF.Copy), reads=[rp], writes=[rso])
                            dst, rd = ((D_gv, R_gv), (D_bv, R_bv), (D_cv, R_cv))[ti - 1]
                            S.dma("pool", lambda e, so=so, rows=rows, dst=dst: e.dma_start(out=dst[rows, :], in_=so), reads=[rso], writes=[rd])
            S.fence()

    def branch_epilogue(es_r, br, h, m, oT, r_oT, do_norm, gidx, post):
        sqr, pn, rr, zr, yo = es_r
        tsl = slice(m * 512, (m + 1) * 512)
        z_t, rz = zr.next()
        S.dma("sp", lambda e: e.dma_start(out=z_t, in_=D_zT[br, h, :, tsl]), reads=[R_zT], writes=[rz])
        y_t, ry = yo.next()
        if do_norm:
            s_t, rsq = sqr.next()
            S.op("act", lambda e: e.activation(out=s_t, in_=oT, func=AF.Square), reads=[r_oT], writes=[rsq])
            p2, rp2 = pn.next()
            S.op("pe", lambda e: e.matmul(p2, lhsT=ones_f, rhs=s_t, start=True, stop=True), reads=[rsq, r_cm], writes=[rp2])
            r_t, rrr = rr.next()
            S.op("act", lambda e: e.activation(out=r_t, in_=p2, func=AF.Ln, scale=1.0 / 128, bias=eps_t[:, 0:1]), reads=[rp2, r_eps], writes=[rrr])
            S.op("act", lambda e: e.activation(out=r_t, in_=r_t, func=AF.Exp, scale=-0.5), reads=[rrr], writes=[rrr])
            sc = cst_t[:, gidx:gidx + 1] if gidx is not None else lam[:, 1:2]
            S.op("dve", lambda e: e.scalar_tensor_tensor(out=r_t, in0=oT, scalar=sc, in1=r_t, op0=ALU.mult, op1=ALU.mult), reads=[r_oT, rrr, r_cst, r_lam], writes=[rrr])
            S.op("dve", lambda e: e.tensor_tensor(out=y_t, in0=r_t, in1=z_t, op=ALU.mult), reads=[rrr, rz], writes=[ry])
        else:
            S.op("dve", lambda e: e.tensor_tensor(out=y_t, in0=oT, in1=z_t, op=ALU.mult), reads=[r_oT, rz], writes=[ry])
        S.dma("pool", lambda e: e.dma_start(out=D_yT[br, h, :, tsl], in_=y_t), reads=[ry], writes=[R_yT[br][h]])

    def epi_rings(es, pfx):
        return (Ring(es, nc, S, pfx + "sq", [128, 512], F32, 2), Ring(es, nc, S, pfx + "pn", [128, 512], F32, 1, psum=True),
                Ring(es, nc, S, pfx + "r", [128, 512], F32, 2), Ring(es, nc, S, pfx + "z", [128, 512], BF16, 2),
                Ring(es, nc, S, pfx + "y", [128, 512], BF16, 2))

    def attention(es, v, br, kT_src, R_kT, v_src, R_v, qT_src, R_qT, ncomp, masked_dsa):
        er = epi_rings(es, f"a{br}_")
        kt_r = Ring(es, nc, S, f"a{br}_k", [128, 8192], BF16, 1)
        v_r = Ring(es, nc, S, f"a{br}_v", [128, 64, 128], BF16, 1)
        q_r = Ring(es, nc, S, f"a{br}_q", [128, ncomp, 2048], BF16, 1)
        if ncomp == 2:
            q0_, rq0_ = q_r.items[0]
            S.op("dve", lambda e: e.memset(q0_, 0.0), writes=[rq0_])
        va = sbt(es, f"a{br}_vis", [128, 4, 512]); r_va = S.res()
        S.dma("sp", lambda e: e.dma_start(out=va, in_=cvar["visA"][v]), writes=[r_va])
        pS = Ring(es, nc, S, f"a{br}_pS", [128, 512], F32, 4 if masked_dsa else 3, psum=True)
        pN = [Ring(es, nc, S, f"a{br}_pN{c}", [128, 512], F32, 1, psum=True) for c in range(ncomp)]
        pD = [Ring(es, nc, S, f"a{br}_pD{c}", [128, 512], F32, 1, psum=True) for c in range(ncomp)]
        LOOKAHEAD = 3 if masked_dsa else 2
        PT = Ring(es, nc, S, f"a{br}_PT", [128, 512], BF16, 6)
        MK = Ring(es, nc, S, f"a{br}_MK", [128, 512], BF16, 8 if masked_dsa else 4)
        oT_r = Ring(es, nc, S, f"a{br}_oT", [128, 512], F32, 2)
        t1_r = Ring(es, nc, S, f"a{br}_t1", [128, 512], F32, 2)
        vv = v_src.rearrange("(kt p) c -> p kt c", p=128)
        KD = 128 // ncomp
        for h in range(4):
            k_t, rk = kt_r.next(); v_t, rv = v_r.next(); q_t, rq = q_r.next()
            for a in range(4):
                S.dma("sp", lambda e, k_t=k_t, h=h, a=a: e.dma_start(out=k_t[:, a * 2048:(a + 1) * 2048], in_=kT_src[h, :, a * 2048:(a + 1) * 2048]), reads=[R_kT[h]], writes=[rk])
                S.dma("act", lambda e, v_t=v_t, h=h, a=a: e.dma_start(out=v_t[:, a * 16:(a + 1) * 16, :], in_=vv[:, a * 16:(a + 1) * 16, h * 128:(h + 1) * 128]), reads=[R_v], writes=[rv])
            if ncomp == 2:
                for c_ in range(2):
                    S.dma("sp", lambda e, q_t=q_t, h=h, c_=c_: e.dma_start(out=q_t[c_ * 64:(c_ + 1) * 64, c_, :], in_=qT_src[h, c_ * 64:(c_ + 1) * 64, :]), reads=[R_qT], writes=[rq])
            else:
                S.dma("sp", lambda e, q_t=q_t, h=h: e.dma_start(out=q_t[:, 0, :], in_=qT_src[h]), reads=[R_qT], writes=[rq])
            for m in range(4):
                nkt = 16 * m + 16 if v == 4 else 4 * (4 * m + v) + 4
                kt_mask0 = 16 * m if v == 4 else 4 * (4 * m + v)
                accs = [(pN[c].next(), pD[c].next()) for c in range(ncomp)]
                pend = []

                def back(item):
                    kt_, c_, p_t_, rpt_ = item
                    (n_, rn), (d_, rd) = accs[c_]
                    S.op("pe", lambda e: e.matmul(n_, lhsT=v_t[:, kt_, :], rhs=p_t_, start=(kt_ == 0), stop=(kt_ == nkt - 1)), reads=[rv, rpt_], writes=[rn])
                    S.op("pe", lambda e: e.matmul(d_, lhsT=ones_b, rhs=p_t_, start=(kt_ == 0), stop=(kt_ == nkt - 1)), reads=[r_cmb, rpt_], writes=[rd])

                for kt in range(nkt):
                    need_mask = (kt >= kt_mask0)
                    if masked_dsa:
                        mk, rmk = MK.next()
                        S.dma("sp", lambda e, mk=mk, m=m, kt=kt: e.dma_start(out=mk, in_=D_nm[m, kt]), reads=[R_nm[m * 4 + a] for a in range(4)], writes=[rmk])
                    elif need_mask:
                        mk, rmk = MK.next()
                        S.op("dve", lambda e, mk=mk, m=m, kt=kt: e.tensor_scalar(out=mk, in0=va[:, m, :], scalar1=float((kt - 16 * m) * 128), scalar2=NEG, op0=ALU.is_le, op1=ALU.mult), reads=[r_va], writes=[rmk])
                    for c in range(ncomp):
                        s, rs = pS.next()
                        use_mask = masked_dsa or need_mask
                        S.op("pe", lambda e, s=s, c=c, kt=kt, m=m, k_t=k_t, q_t=q_t, use_mask=use_mask: e.matmul(s, lhsT=k_t[:, kt * 128:(kt + 1) * 128], rhs=q_t[:, c, m * 512:(m + 1) * 512], start=True, stop=not use_mask), reads=[rk, rq], writes=[rs])
                        if use_mask:
                            S.op("pe", lambda e, s=s, mk=mk: e.matmul(s, lhsT=ident_b, rhs=mk, start=False, stop=True), reads=[rmk, r_cmb], writes=[rs])
                        p_t, rpt = PT.next()
                        S.op("act", lambda e, s=s, p_t=p_t: e.activation(out=p_t, in_=s, func=AF.Exp, bias=-8.0), reads=[rs], writes=[rpt])
                        pend.append((kt, c, p_t, rpt))
                        if len(pend) > LOOKAHEAD:
                            back(pend.pop(0))
                while pend:
                    back(pend.pop(0))
                o_t, ro = oT_r.next()
                (n_, rn), (d_, rd) = accs[0]
                t1, rt1 = t1_r.next()
                S.op("act", lambda e, t1=t1, d_=d_: e.activation(out=t1, in_=d_, func=AF.Ln), reads=[rd], writes=[rt1])
                S.op("act", lambda e, t1=t1: e.activation(out=t1, in_=t1, func=AF.Exp, scale=-1.0), reads=[rt1], writes=[rt1])
                S.op("dve", lambda e, o_t=o_t, n_=n_, t1=t1: e.tensor_tensor(out=o_t, in0=n_, in1=t1, op=ALU.mult), reads=[rn, rt1], writes=[ro])
                if ncomp == 2:
                    (n1, rn1), (d1, rd1) = accs[1]
                    t2, rt2 = t1_r.next()
                    S.op("act", lambda e, t2=t2, d1=d1: e.activation(out=t2, in_=d1, func=AF.Ln), reads=[rd1], writes=[rt2])
                    S.op("act", lambda e, t2=t2: e.activation(out=t2, in_=t2, func=AF.Exp, scale=-1.0), reads=[rt2], writes=[rt2])
                    S.op("dve", lambda e, t2=t2, n1=n1: e.tensor_tensor(out=t2, in0=n1, in1=t2, op=ALU.mult), reads=[rn1, rt2], writes=[rt2])
                    S.op("dve", lambda e, o_t=o_t, t2=t2: e.scalar_tensor_tensor(out=o_t, in0=t2, scalar=lam[:, 0:1], in1=o_t, op0=ALU.mult, op1=ALU.add), reads=[rt2, ro, r_lam], writes=[ro])
                branch_epilogue(er, br, h, m, o_t, ro, ncomp == 2, None, None)
        S.fence()

    def rest(l, v, xloader, writer):
        with ExitStack() as es:
            hT = sbt(es, "hT2", [128, 8, 2048], BF16); r_hT = S.res()
            with ExitStack() as es2:
                for _ in make_h(es2, xloader, 4, hT, r_hT, multi=True):
                    pass
                S.fence()
            ps = Ring(es, nc, S, "p2", [128, 512], F32, 4, psum=True)
            pn = Ring(es, nc, S, "p2n", [128, 512], F32, 2, psum=True)
            sqr = Ring(es, nc, S, "n2_sq", [128, 512], F32, 2)
            rr = Ring(es, nc, S, "n2_r", [128, 512], F32, 2)
            stb = Ring(es, nc, S, "stb2", [128, 512], BF16, 6)
            stf = Ring(es, nc, S, "stf2", [128, 512], F32, 5)
            wst = Ring(es, nc, S, "w2st", [128, 8, 512], F32, 2)
            wbr = Ring(es, nc, S, "w2b", [128, 8, 512], BF16, 2)
            wv = W[l]["w_q"].rearrange("(kc p) n -> p kc n", p=128)
            for ct in range(13):
                c0 = ct * 512
                n = 512 if ct < 12 else 384
                w_t, rw = wst.next()
                S.dma("act", lambda e, w_t=w_t, c0=c0, n=n: e.dma_start(out=w_t[:, :, 0:n], in_=wv[:, :, c0:c0 + n]), writes=[rw])
                wb, rwb = wbr.next()
                for kc in range(8):
                    if kc % 2:
                        S.op("act", lambda e, w_t=w_t, wb=wb, kc=kc, n=n: e.activation(out=wb[:, kc, 0:n], in_=w_t[:, kc, 0:n], func=AF.Copy, scale=cst_t[:, kc:kc + 1]), reads=[rw, r_cst], writes=[rwb])
                    else:
                        S.op("dve", lambda e, w_t=w_t, wb=wb, kc=kc, n=n: e.tensor_scalar(out=wb[:, kc, 0:n], in0=w_t[:, kc, 0:n], scalar1=cst_t[:, kc:kc + 1], scalar2=None, op0=ALU.mult), reads=[rw, r_cst], writes=[rwb])
                for cc in range(n // 128):
                    ch = ct * 4 + cc
                    if ch == 50:
                        for blk in range(16):
                            p, rp = ps.next()
                            for kc in range(8):
                                S.op("pe", lambda e, p=p, kc=kc, blk=blk, wb=wb, cc=cc: e.matmul(p[:, 0:8], lhsT=hT[:, kc, blk * 128:(blk + 1) * 128], rhs=wb[:, kc, cc * 128:cc * 128 + 8], start=(kc == 0), stop=(kc == 7)), reads=[rwb, r_hT], writes=[rp])
                            S.op("act", lambda e, p=p, blk=blk: e.activation(out=iw_t[:, blk, :], in_=p[:, 0:8], func=AF.Copy, scale=8 ** -0.5), reads=[rp], writes=[r_iw])
                        continue
                    for tg in range(4):
                        tsl = slice(tg * 512, (tg + 1) * 512)
                        p, rp = ps.next()
                        for kc in range(8):
                            S.op("pe", lambda e, p=p, kc=kc, wb=wb, cc=cc, tsl=tsl: e.matmul(p, lhsT=wb[:, kc, cc * 128:(cc + 1) * 128], rhs=hT[:, kc, tsl], start=(kc == 0), stop=(kc == 7)), reads=[rwb, r_hT], writes=[rp])
                        if ch < 26:
                            so, rso = stb.next()
                            if ch < 2:
                                S.op("act", lambda e, so=so, p=p: e.activation(out=so, in_=p, func=AF.Copy, scale=0.125), reads=[rp], writes=[rso])
                                dst, rd = D_gqT[ch, :, tsl], R_gqT
                            elif ch < 6:
                                fm_norm((sqr, pn, rr), p, rp, 128, 11, 128 ** -0.5, so, rso)
                                dst, rd = D_bqT[ch - 2, :, tsl], R_bqT
                            elif ch < 10:
                                S.op("act", lambda e, so=so, p=p: e.activation(out=so, in_=p, func=AF.Copy, scale=0.125), reads=[rp], writes=[rso])
                                dst, rd = D_iqT[ch - 6, :, tsl], R_iqT
                            elif ch < 14:
                                fm_norm((sqr, pn, rr), p, rp, 64, 10, 0.125, so, rso)
                                dst, rd = D_cqT[ch - 10, :, tsl], R_cqT
                            else:
                                S.op("act", lambda e, so=so, p=p: e.activation(out=so, in_=p, func=AF.Silu), reads=[rp], writes=[rso])
                                dst, rd = D_zT[(ch - 14) // 4, (ch - 14) % 4, :, tsl], R_zT
                            S.dma("pool", lambda e, so=so, dst=dst: e.dma_start(out=dst, in_=so), reads=[rso], writes=[rd])
                        else:
                            sf, rsf = stf.next()
                            S.op("act", lambda e, sf=sf, p=p: e.activation(out=sf, in_=p, func=AF.Sigmoid), reads=[rp], writes=[rsf])
                            S.dma("pool", lambda e, sf=sf, ch=ch, tsl=tsl: e.dma_start(out=D_gT[ch - 26, :, tsl], in_=sf), reads=[rsf], writes=[R_gT])
            S.fence()

        with ExitStack() as es:
            er = epi_rings(es, "g_")
            val_t = sbt(es, "val_t", [128, 128]); r_val = S.res()
            S.dma("sp", lambda e: e.dma_start(out=val_t, in_=cvar["valid"][v]), writes=[r_val])
            qT = sbt(es, "g_qT", [64, 4, 2048], BF16); r_qT = S.res()
            S.dma("sp", lambda e: e.dma_start(out=qT, in_=D_gqT.rearrange("a (b p) t -> p (a b) t", b=2)), reads=[R_gqT], writes=[r_qT])
            St = sbt(es, "g_S", [64, 4, 128]); r_S = S.res()
            Sb = Ring(es, nc, S, "g_Sb", [64, 4, 128], BF16, 2)
            S.op("dve", lambda e: e.memset(St, 0.0), writes=[r_S])
            kr = Ring(es, nc, S, "g_k", [128, 8, 256], BF16, 2)
            vr = Ring(es, nc, S, "g_v", [128, 8, 512], BF16, 2)
            pu = Ring(es, nc, S, "g_pu", [64, 4, 128], F32, 4, psum=True)
            po = Ring(es, nc, S, "g_po", [128, 4, 64], F32, 2, psum=True)
            acc = Ring(es, nc, S, "g_acc", [128, 4, 8, 64], F32, 2)
            kv = D_kdec.rearrange("(b p) c -> p b c", p=128)
            vv = D_gv.rearrange("(b p) c -> p b c", p=128)
            for c in range(128):
                if c % 16 == 0:
                    k_t, rk = kr.next(); v_t, rv = vr.next()
                    b0 = c // 2
                    S.dma("sp", lambda e, k_t=k_t, b0=b0: e.dma_start(out=k_t, in_=kv[:, b0:b0 + 8, :]), reads=[R_kdec], writes=[rk])
                    S.dma("sp", lambda e, v_t=v_t, b0=b0: e.dma_start(out=v_t, in_=vv[:, b0:b0 + 8, :]), reads=[R_gv], writes=[rv])
                if c % 32 == 0:
                    a_t, ra = acc.next()
                    S.op("pool", lambda e, a_t=a_t: e.memset(a_t, 0.0), writes=[ra])
                bi = (c % 16) // 2
                pb = (c % 2) * 64
                u, ru = pu.next()
                for h in range(4):
                    S.op("pe", lambda e, u=u, h=h, k_t=k_t, v_t=v_t, bi=bi, pb=pb: e.matmul(u[:, h, :], lhsT=k_t[pb:pb + 64, bi, h * 64:(h + 1) * 64], rhs=v_t[pb:pb + 64, bi, h * 128:(h + 1) * 128], start=True, stop=True), reads=[rk, rv], writes=[ru])
                S.op("dve", lambda e, c=c: e.tensor_tensor(out=St, in0=St, in1=dec[:, :, c:c + 1].to_broadcast([64, 4, 128]), op=ALU.mult), reads=[r_S, r_dec], writes=[r_S])
                S.op("dve", lambda e, u=u: e.tensor_tensor(out=St, in0=St, in1=u, op=ALU.add), reads=[r_S, ru], writes=[r_S])
                m = c // 32
                lc = c % 8
                if v == 4 or (c - 32 * m) // 8 == v:
                    s_b, rsb = Sb.next()
                    S.op("act", lambda e, s_b=s_b: e.activation(out=s_b, in_=St, func=AF.Copy), reads=[r_S], writes=[rsb])
                    o, ro = po.next()
                    t0 = m * 512 + lc * 64
                    for h in range(4):
                        S.op("pe", lambda e, o=o, h=h, s_b=s_b, t0=t0: e.matmul(o[:, h, :], lhsT=s_b[:, h, :], rhs=qT[:, h, t0:t0 + 64], start=True, stop=True), reads=[rsb, r_qT], writes=[ro])
                    S.op("dve", lambda e, o=o, a_t=a_t, lc=lc, c=c: e.scalar_tensor_tensor(out=a_t[:, :, lc, :], in0=o, scalar=val_t[:, c:c + 1], in1=a_t[:, :, lc, :], op0=ALU.mult, op1=ALU.add), reads=[ro, ra, r_val], writes=[ra])
                if c % 32 == 31:
                    for h in range(4):
                        branch_epilogue(er, 0, h, m, a_t[:, h, :, :].rearrange("p a b -> p (a b)"), ra, True, 12, None)
            S.fence()

        with ExitStack() as es:
            attention(es, v, 2, D_ckT, R_ckT, D_cv, R_cv, D_cqT, R_cqT, 2, False)

        with ExitStack() as es:
            iq = sbt(es, "i_q", [128, 8, 2048], BF16); r_iq = S.res()
            S.op("pool", lambda e: e.memset(iq, 0.0), writes=[r_iq])
            for h_ in range(8):
                hb_ = (h_ % 2) * 64
                S.dma("sp", lambda e, h_=h_, hb_=hb_: e.dma_start(out=iq[hb_:hb_ + 64, h_, :], in_=D_iqT[h_ // 2, hb_:hb_ + 64, :]), reads=[R_iqT], writes=[r_iq])
            io_t = sbt(es, "i_iota", [128, 2048]); r_io = S.res()
            S.dma("sp", lambda e: e.dma_start(out=io_t, in_=iota), writes=[r_io])
            vr_t = sbt(es, "i_vr", [128, 16]); r_vr = S.res()
            S.dma("sp", lambda e: e.dma_start(out=vr_t, in_=cvar["vis_rel"][v]), writes=[r_vr])
            thr_all = sbt(es, "i_thr", [128, 16]); r_thr = S.res()
            cnt_all = sbt(es, "i_cnt", [128, 16]); r_cnt = S.res()
            sc_r = Ring(es, nc, S, "i_sc", [128, 8192], F32, 3)
            nm_r = Ring(es, nc, S, "i_nm", [128, 8192], BF16, 1)
            junk = Ring(es, nc, S, "i_junk", [128, 4608], BF16, 1)
            junka = Ring(es, nc, S, "i_junka", [128, 3840], BF16, 1)
            wtab = Ring(es, nc, S, "i_wtab", [128, K_IT], F32, 2)
            cact = Ring(es, nc, S, "i_cact", [128, 1], F32, 4)
            p2_t = sbt(es, "i_p2", [128, K_IT]); r_p2 = S.res()
            S.dma("sp", lambda e: e.dma_start(out=p2_t, in_=pow2_in), writes=[r_p2])
            pd = Ring(es, nc, S, "i_pd", [128, 512], F32, 4, psum=True)
            pacc = Ring(es, nc, S, "i_pacc", [128, 512], F32, 2, psum=True)
            dgr = Ring(es, nc, S, "i_dg", [128, 8, 128], BF16, 2)
            absw_t = sbt(es, "i_absw", [128, 16, 8]); sgn_t = sbt(es, "i_sgn", [128, 16, 8], BF16); r_sgn = S.res()
            S.op("act", lambda e: e.activation(out=absw_t, in_=iw_t, func=AF.Abs), reads=[r_iw], writes=[r_sgn])
            S.op("act", lambda e: e.activation(out=sgn_t, in_=iw_t, func=AF.Sign), reads=[r_iw, r_sgn], writes=[r_sgn])
            ptr = Ring(es, nc, S, "i_pt", [128, 4, 128], BF16, 2, psum=True)
            rl = Ring(es, nc, S, "i_rl", [128, 512], BF16, 5)
            tro = Ring(es, nc, S, "i_tro", [128, 4, 128], BF16, 5)
            bs = Ring(es, nc, S, "i_bs", [128, 8], F32, 2)
            def s1_gen(qb, st):
                m = qb // 4
                nk = 2048 * (m + 1) if v == 4 else 512 * (4 * m + v + 1)
                nfill = nk - 2048 * m
                sc, rsc = sc_r.next()
                st.update(qb=qb, m=m, nk=nk, nfill=nfill, sc=sc, rsc=rsc)
                dg, rdg = dgr.next()
                S.op("dve", lambda e, dg=dg, qb=qb: e.tensor_tensor(out=dg, in0=ident_b.rearrange("p (o q) -> p o q", o=1).to_broadcast([128, 8, 128]), in1=sgn_t[:, qb, :].rearrange("p (h o) -> p h o", o=1).to_broadcast([128, 8, 128]), op=ALU.mult), reads=[r_cmb, r_sgn], writes=[rdg])
                for kg in range(nk // 512):
                    pa, rpa = pacc.next()
                    pend = []

                    def back(item):
                        h_, r_t_, rr__ = item
                        S.op("pe", lambda e: e.matmul(pa, lhsT=dg[:, h_, :], rhs=r_t_, start=(h_ == 0), stop=(h_ == 7)), reads=[rdg, rr__], writes=[rpa])

                    for h in range(8):
                        d, rd = pd.next()
                        hb = (h % 2) * 64
                        S.op("pe", lambda e, d=d, h=h, hb=hb, qb=qb, kg=kg: e.matmul(d, lhsT=iq[:, h, qb * 128:(qb + 1) * 128], rhs=ikT[:, kg * 512:(kg + 1) * 512], start=True, stop=True), reads=[r_iq, r_ikT], writes=[rd])
                        r_t, rr_ = rl.next()
                        S.op("act", lambda e, d=d, r_t=r_t, qb=qb, h=h: e.activation(out=r_t, in_=d, func=AF.Relu, scale=absw_t[:, qb, h:h + 1]), reads=[rd, r_sgn], writes=[rr_])
                        pend.append((h, r_t, rr_))
                        if len(pend) > 2:
                            back(pend.pop(0))
                    while pend:
                        back(pend.pop(0))
                    dst = sc[:, kg * 512:(kg + 1) * 512]
                    if kg % 2:
                        S.op("act", lambda e, dst=dst, pa=pa: e.activation(out=dst, in_=pa, func=AF.Copy), reads=[rpa], writes=[rsc])
                    else:
                        S.op("dve", lambda e, dst=dst, pa=pa: e.tensor_copy(out=dst, in_=pa), reads=[rpa], writes=[rsc])
                    yield

            def qb_gen(st):
                qb = st["qb"]; m = st["m"]; nk = st["nk"]; nfill = st["nfill"]; sc = st["sc"]; rsc = st["rsc"]
                b, rb_ = bs.next()
                S.op("dve", lambda e, b=b, sc=sc, nk=nk: e.tensor_reduce(out=b[:, 0:1], in_=sc[:, 0:nk], axis=AX.X, op=ALU.min), reads=[rsc], writes=[rb_])
                S.op("dve", lambda e, b=b, sc=sc, nk=nk: e.tensor_reduce(out=b[:, 1:2], in_=sc[:, 0:nk], axis=AX.X, op=ALU.max), reads=[rsc, rb_], writes=[rb_])
                S.op("dve", lambda e, b=b: e.scalar_tensor_tensor(out=b[:, 1:2], in0=b[:, 1:2], scalar=1.0, in1=b[:, 0:1], op0=ALU.add, op1=ALU.subtract), reads=[rb_], writes=[rb_])
                wt, rwt = wtab.next()
                S.op("dve", lambda e, b=b, wt=wt: e.tensor_scalar(out=wt, in0=p2_t, scalar1=b[:, 1:2], scalar2=None, op0=ALU.mult), reads=[rb_, r_p2], writes=[rwt])
                S.op("dve", lambda e, sc=sc, nk=nk, qb=qb, nfill=nfill: e.scalar_tensor_tensor(out=sc[:, nk - nfill:nk], in0=io_t[:, 0:nfill], scalar=vr_t[:, qb:qb + 1], in1=sc[:, nk - nfill:nk], op0=ALU.subtract, op1=ALU.min), reads=[rsc, r_io, r_vr, rb_], writes=[rsc])
                jk, rj = junk.next()
                ja, rja = junka.next()
                nh = (nk * 9 // 16) // 128 * 128
                r_mid = S.res(); r_cd = S.res(); r_u = S.res()
                S.op("dve", lambda e, b=b, wt=wt: e.tensor_tensor(out=b[:, 2:3], in0=b[:, 0:1], in1=wt[:, 0:1], op=ALU.add), reads=[rb_, rwt], writes=[r_mid])
                yield
                for it in range(K_IT):
                    S.op("dve", lambda e, b=b, sc=sc, jk=jk, nh=nh: e.tensor_scalar(out=jk[:, 0:nh], in0=sc[:, 0:nh], scalar1=b[:, 2:3], scalar2=0.0, op0=ALU.is_ge, op1=ALU.add, accum_out=b[:, 3:4]), reads=[rsc, r_mid], writes=[rj, r_cd])
                    ca, rca = cact.next()
                    S.op("act", lambda e, b=b, sc=sc, ja=ja, nh=nh, nk=nk, ca=ca: e.activation(out=ja[:, 0:nk - nh], in_=sc[:, nh:nk], func=AF.Sign, scale=-1.0, bias=b[:, 2:3], accum_out=ca[:, 0:1]), reads=[rsc, r_mid], writes=[rja, rca])
                    S.op("dve", lambda e, b=b, ca=ca: e.scalar_tensor_tensor(out=b[:, 4:5], in0=b[:, 3:4], scalar=2.0, in1=ca[:, 0:1], op0=ALU.mult, op1=ALU.subtract), reads=[r_cd, rca], writes=[r_u])
                    S.op("dve", lambda e, b=b, wt=wt, it=it, nk=nk, nh=nh: e.tensor_scalar(out=b[:, 5:6], in0=b[:, 4:5], scalar1=float(2 * TOPK - 1 - (nk - nh)), scalar2=wt[:, it:it + 1], op0=ALU.is_ge, op1=ALU.mult), reads=[r_u, rwt], writes=[r_u])
                    nx = it + 1 if it < K_IT - 1 else it
                    S.op("dve", lambda e, b=b, wt=wt, nx=nx: e.scalar_tensor_tensor(out=b[:, 2:3], in0=b[:, 2:3], scalar=wt[:, nx:nx + 1], in1=b[:, 5:6], op0=ALU.subtract, op1=ALU.add), reads=[r_mid, r_u, rwt], writes=[r_mid])
                    yield
                S.op("dve", lambda e, b=b: e.tensor_copy(out=b[:, 0:1], in_=b[:, 2:3]), reads=[r_mid, rb_], writes=[rb_])
                if debug:
                    pass
                    S.op("dve", lambda e, b=b, qb=qb: e.tensor_copy(out=thr_all[:, qb:qb + 1], in_=b[:, 0:1]), reads=[rb_], writes=[r_thr])
                nm, rnm = nm_r.next()
                S.op("dve", lambda e, nm=nm, sc=sc, b=b, nk=nk: e.tensor_scalar(out=nm[:, 0:nk], in0=sc[:, 0:nk], scalar1=b[:, 0:1], scalar2=NEG, op0=ALU.is_lt, op1=ALU.mult), reads=[rsc, rb_], writes=[rnm])
                for g in range(nk // 512):
                    pt, rpt = ptr.next()
                    for a in range(4):
                        S.op("pe", lambda e, pt=pt, a=a, g=g, nm=nm: e.transpose(pt[:, a, :], nm[:, (g * 4 + a) * 128:(g * 4 + a + 1) * 128], ident_b), reads=[rnm, r_cmb], writes=[rpt])
                    to, rto = tro.next()
                    S.op("dve", lambda e, to=to, pt=pt: e.tensor_copy(out=to, in_=pt), reads=[rpt], writes=[rto])
                    S.dma("pool", lambda e, to=to, qb=qb, g=g: e.dma_start(out=D_nm[qb // 4, g * 4:g * 4 + 4, :, (qb % 4) * 128:(qb % 4 + 1) * 128].rearrange("a p q -> p a q"), in_=to), reads=[rto], writes=[R_nm[qb]])
                yield
            def run_all(g):
                for _ in g:
                    pass

            sts = [dict() for _ in range(16)]
            run_all(s1_gen(0, sts[0]))
            run_all(s1_gen(1, sts[1]))
            for pair in range(8):
                qa, qb_ = 2 * pair, 2 * pair + 1
                gens = [qb_gen(sts[qa]), qb_gen(sts[qb_])]
                for g in gens:
                    next(g)
                nxt = s1_gen(qa + 2, sts[qa + 2]) if qa + 2 < 16 else None
                for it in range(K_IT):
                    for g in gens:
                        next(g)
                    if nxt is not None:
                        try:
                            next(nxt)
                        except StopIteration:
                            nxt = None
                if nxt is not None:
                    run_all(nxt)
                next(gens[0])
                if qa + 3 < 16:
                    run_all(s1_gen(qa + 3, sts[qa + 3]))
                next(gens[1])
            if debug:
                S.dma("pool", lambda e: e.dma_start(out=dbg["thr"], in_=thr_all), reads=[r_thr], writes=[newout()])
                S.dma("pool", lambda e: e.dma_start(out=dbg["cnt"], in_=cnt_all), reads=[r_cnt], writes=[newout()])
            S.fence()

        with ExitStack() as es:
            attention(es, v, 1, D_bkT, R_bkT, D_bv, R_bv, D_bqT, R_bqT, 1, True)

        with ExitStack() as es:
            wbr_b = sbt(es, "t_wbr", [128, 12, 1024], BF16); r_wbr = S.res()
            wo_b = sbt(es, "t_wo", [128, 8, 1024], BF16); r_wo = S.res()
            with ExitStack() as es2:
                st = Ring(es2, nc, S, "t_wst", [128, 4, 1024], F32, 2)
                for i in range(3):
                    s_t, rs = st.next()
                    S.dma("sp", lambda e, s_t=s_t, i=i: e.dma_start(out=s_t, in_=W[l]["w_br"][i].rearrange("(kc p) n -> p kc n", p=128)), writes=[rs])
                    S.op("dve", lambda e, s_t=s_t, i=i: e.tensor_copy(out=wbr_b[:, i * 4:(i + 1) * 4, :], in_=s_t), reads=[rs], writes=[r_wbr])
                for i in range(2):
                    s_t, rs = st.next()
                    S.dma("sp", lambda e, s_t=s_t, i=i: e.dma_start(out=s_t, in_=W[l]["w_out"][i * 512:(i + 1) * 512].rearrange("(kc p) n -> p kc n", p=128)), writes=[rs])
                    S.op("dve", lambda e, s_t=s_t, i=i: e.tensor_copy(out=wo_b[:, i * 4:(i + 1) * 4, :], in_=s_t), reads=[rs], writes=[r_wo])
                S.fence()
            yT = Ring(es, nc, S, "t_yT", [128, 12, 512], BF16, 2)
            gT = Ring(es, nc, S, "t_gT", [128, 3, 512], F32, 4)
            mT = Ring(es, nc, S, "t_mT", [128, 8, 512], BF16, 2)
            pb = Ring(es, nc, S, "t_pb", [128, 512], F32, 4, psum=True)
            po = Ring(es, nc, S, "t_po", [128, 512], F32, 2, psum=True)
            ot = Ring(es, nc, S, "t_ot", [128, 512], F32, 3)
            xc7 = Ring(es, nc, S, "t_xc", [128, 8, 512], F32, 1)
            accr = Ring(es, nc, S, "t_acc", [128, 512], F32, 3)
            tmp = Ring(es, nc, S, "t_tmp", [128, 512], F32, 3)
            xr = Ring(es, nc, S, "t_x", [128, 8, 512], F32, 1)
            for tg in range(4):
                tsl = slice(tg * 512, (tg + 1) * 512)
                y_t, ry = yT.next()
                S.dma("sp", lambda e, y_t=y_t, tsl=tsl: e.dma_start(out=y_t, in_=D_yT[:, :, :, tsl].rearrange("b k p t -> p (b k) t")), reads=[r for rr_ in R_yT for r in rr_], writes=[ry])
                m_t, rm = mT.next()
                for dc in range(8):
                    g_t, rg = gT.next()
                    S.dma("act", lambda e, g_t=g_t, dc=dc, tsl=tsl: e.dma_start(out=g_t, in_=D_gT.rearrange("(b d) p t -> d p b t", b=3)[dc, :, :, tsl]), reads=[R_gT], writes=[rg])
                    a_t, ra = accr.next()
                    for br in range(3):
                        p, rp = pb.next()
                        for kc in range(4):
                            S.op("pe", lambda e, p=p, br=br, kc=kc, dc=dc, y_t=y_t: e.matmul(p, lhsT=wbr_b[:, br * 4 + kc, dc * 128:(dc + 1) * 128], rhs=y_t[:, br * 4 + kc, :], start=(kc == 0), stop=(kc == 3)), reads=[r_wbr, ry], writes=[rp])
                        if br == 0:
                            S.op("dve", lambda e, a_t=a_t, p=p, g_t=g_t: e.tensor_tensor(out=a_t, in0=p, in1=g_t[:, 0, :], op=ALU.mult), reads=[rp, rg], writes=[ra])
                        else:
                            t_t, rt = tmp.next()
                            S.op("dve", lambda e, t_t=t_t, p=p, g_t=g_t, br=br: e.tensor_tensor(out=t_t, in0=p, in1=g_t[:, br, :], op=ALU.mult), reads=[rp, rg], writes=[rt])
                            if br == 1:
                                S.op("pool", lambda e, a_t=a_t, t_t=t_t: e.tensor_tensor(out=a_t, in0=a_t, in1=t_t, op=ALU.add), reads=[ra, rt], writes=[ra])
                            else:
                                S.op("pool", lambda e, a_t=a_t, t_t=t_t, m_t=m_t, dc=dc: e.tensor_tensor(out=m_t[:, dc, :], in0=a_t, in1=t_t, op=ALU.add), reads=[ra, rt], writes=[rm])
                x_t, rx = xr.next()
                xloader(x_t, rx, tg, xc7)
                for dc2 in range(8):
                    o, ro = po.next()
                    for dc in range(8):
                        S.op("pe", lambda e, o=o, dc2=dc2, dc=dc, m_t=m_t: e.matmul(o, lhsT=wo_b[:, dc, dc2 * 128:(dc2 + 1) * 128], rhs=m_t[:, dc, :], start=(dc == 0), stop=(dc == 7)), reads=[rm, r_wo], writes=[ro])
                    o_t, rot = ot.next()
                    S.op("dve", lambda e, o_t=o_t, x_t=x_t, o=o, dc2=dc2: e.tensor_tensor(out=o_t, in0=x_t[:, dc2, :], in1=o, op=ALU.add), reads=[rx, ro], writes=[rot])
                    writer(o_t, rot, dc2, tg)
            S.fence()

    xvf = xT_full.rearrange("(kc p) t -> p kc t", p=128)
    x1v = D_x1T.rearrange("(kc p) t -> p kc t", p=128)

    def full_loader(xv, R_src):
        def f(x_t, rx, tg, rings=None):
            S.dma("sp", lambda e: e.dma_start(out=x_t, in_=xv[:, :, tg * 512:(tg + 1) * 512]), reads=[R_src] if R_src else [], writes=[rx])
        return f

    def pass_loader(xv, R_src, p):
        def f(x_t, rx, tg, rings=None):
            sb_ = 4 * tg + p
            S.dma("sp", lambda e: e.dma_start(out=x_t, in_=xv[:, :, sb_ * 512:(sb_ + 1) * 512]), reads=[R_src] if R_src else [], writes=[rx])
        return f

    def blend_loader(xv, R_src):
        def f(x_t, rx, tg, rings):
            for p in range(4):
                sb_ = 4 * tg + p
                if p == 0:
                    c_t, rc = x_t, rx
                else:
                    c_t, rc = rings.next()
                S.dma("sp", lambda e: e.dma_start(out=c_t, in_=xv[:, :, sb_ * 512:(sb_ + 1) * 512]), reads=[R_src] if R_src else [], writes=[rc])
                if p == 0:
                    S.op("dve", lambda e: e.tensor_scalar(out=x_t, in0=x_t, scalar1=sel_t[:, 0:1], scalar2=None, op0=ALU.mult), reads=[rx, r_sel], writes=[rx])
                else:
                    S.op("dve", lambda e: e.scalar_tensor_tensor(out=x_t, in0=c_t, scalar=sel_t[:, p:p + 1], in1=x_t, op0=ALU.mult, op1=ALU.add), reads=[rc, rx, r_sel], writes=[rx])
        return f

    def x1_writer(p):
        def f(o_t, ro, dc2, tg):
            sb_ = 4 * tg + p
            S.dma("pool", lambda e: e.dma_start(out=D_x1T[dc2 * 128:(dc2 + 1) * 128, sb_ * 512:(sb_ + 1) * 512], in_=o_t), reads=[ro], writes=[R_x1T])
        return f

    def out_writer(o_t, ro, dc2, tg):
        S.dma("pool", lambda e: e.dma_start(out=outT[dc2 * 128:(dc2 + 1) * 128, tg * 512:(tg + 1) * 512], in_=o_t), reads=[ro], writes=[newout()])

    if fused:
        la, lb = layers
        layer_setup(la)
        kside(la, full_loader(xvf, None))
        for p in range(4):
            rest(la, p, pass_loader(xvf, None, p), x1_writer(p))
        layer_setup(lb)
        kside(lb, full_loader(x1v, R_x1T))
        rest(lb, 4, blend_loader(x1v, R_x1T), out_writer)
    else:
        la = layers[0]
        layer_setup(la)
        kside(la, full_loader(xvf, None))
        rest(la, 4, blend_loader(xvf, None), out_writer)
    S.wait_all("sp", outs)
    S.fence()

    n = S.finalize()
    top.close()
    return n


def _tok_idx(j):
    return np.concatenate([np.arange((4 * m + j) * 512, (4 * m + j + 1) * 512) for m in range(4)])


def _var_consts(j):
    own = _tok_idx(j)
    vis_end = (own // 64 + 1) * 64
    vis_rel = np.zeros((128, 16), np.float32)
    for qb in range(16):
        m = qb // 4
        vis_rel[:, qb] = (vis_end[qb * 128:(qb + 1) * 128] - 2048 * m - 0.5) * (-1e30)
    visA = np.zeros((128, 4, 512), np.float32)
    for m in range(4):
        visA[:, m, :] = vis_end[m * 512:(m + 1) * 512][None, :] - 2048 * m - np.arange(128)[:, None]
    valid = np.zeros((128, 128), np.float32)
    for c in range(128):
        m = c // 32
        if (c - 32 * m) // 8 == j:
            valid[:, c] = 1.0
    return vis_rel, visA, valid


def _const_inputs(j):
    cm = np.zeros((128, 4, 128), np.float32)
    cm[:, 0, :] = 1.0
    cm[:64, 1, :64] = 1.0
    cm[64:, 1, 64:] = 1.0
    for s in range(128):
        for t in range(128):
            if s // 64 == t // 64 and s > t:
                cm[s, 2, t] = 1.0
    cm[:, 3, :] = np.eye(128, dtype=np.float32)
    ind = np.zeros((128, 2), np.float32)
    ind[:64, 0] = 1
    ind[64:, 1] = 1
    iota = np.tile((np.arange(2048, dtype=np.float64) * (-1e30)).astype(np.float32)[None, :], (128, 1))
    pow2 = np.tile((2.0 ** -(np.arange(K_IT) + 1.0)).astype(np.float32)[None, :], (128, 1))
    vs = [_var_consts(p) for p in (0, 1, 2, 3, j)]
    sel = np.zeros((128, 4), np.float32)
    sel[:, j] = 1.0
    return {"cmat": cm, "ind": ind, "iota": iota, "vis_rel": np.stack([v[0] for v in vs]),
            "visA": np.stack([v[1] for v in vs]), "valid": np.stack([v[2] for v in vs]), "sel": sel, "pow2": pow2}


def _layer_weights(P, l):
    lam_init = 0.8 - 0.6 * float(np.exp(-0.3 * l))
    w = P["w_in"][l]
    cst = np.zeros((128, 16), np.float32)
    cst[:, 0:8] = P["norm_g"][l].reshape(8, 128).T
    cst[:, 8] = np.tile(P["diff_kn_g"][l], 2)
    cst[:, 9] = P["dsa_kn_g"][l]
    cst[:, 10] = np.tile(P["diff_qn_g"][l], 2)
    cst[:, 11] = P["dsa_qn_g"][l]
    cst[:, 12] = P["gla_norm_g"][l]
    cst[:, 13] = lam_init
    lqk = np.tile(np.concatenate([P["diff_lq1"][l], P["diff_lk1"][l], P["diff_lq2"][l], P["diff_lk2"][l]])[None, :], (128, 1))
    d = {f"w_k{l}": w[:, wk_cols()], f"w_q{l}": w[:, wq_cols()], f"w_br{l}": P["w_br"][l], f"w_out{l}": P["w_out"][l],
         f"cst{l}": cst, f"wa2{l}": P["gla_wa2"][l], f"ba_bc{l}": np.tile(P["gla_ba"][l][None, :], (128, 1)), f"lqk{l}": lqk}
    return {k: np.ascontiguousarray(v, dtype=np.float32) for k, v in d.items()}


def core_inputs(xb, j, P, layers=(0, 1)):
    d = {"xT_full": np.ascontiguousarray(xb.T, dtype=np.float32)}
    for l in layers:
        d.update(_layer_weights(P, l))
    d.update({k: np.ascontiguousarray(v, dtype=np.float32) for k, v in _const_inputs(j).items()})
    return d


_NC = {}


def _get_nc(layers, fused):
    key = (tuple(layers), fused)
    if key not in _NC:
        nc = bass.Bass("TRN2", target_bir_lowering=False)
        build_fused(nc, layers=layers, fused=fused)
        _NC[key] = nc
    return _NC[key]


def kernel(**inputs):
    P = {k: np.asarray(v, dtype=np.float32) for k, v in inputs.items()}
    x = P["x"]
    nc = _get_nc((0, 1), True)
    wts = {}
    for l in (0, 1):
        wts.update(_layer_weights(P, l))
    in_maps = []
    for c in range(8):
        d = {"xT_full": np.ascontiguousarray(x[c // 4].T, dtype=np.float32)}
        d.update(wts)
        d.update({k: np.ascontiguousarray(v, dtype=np.float32) for k, v in _const_inputs(c % 4).items()})
        in_maps.append(d)
    res = run_bass_kernel_spmd(nc, in_maps, core_ids=list(range(8)))
    out = np.empty_like(x)
    for c in range(8):
        out[c // 4][_tok_idx(c % 4)] = res.results[c]["outT"].T
    return out
```
